# Optimizing a Trainium2 kernel written in Bass

```python
import math
import jax
import jax.numpy as jnp
from jax import lax
import numpy as np

D_MODEL = 1024
BATCH = 16
SEQ = 256
DEPTH = 2
DEC_BATCH = 4
DEC_SEQ = 1024
PAST_LEN = 512

GRID_W = 64
N_EVEN = (DEPTH + 1) // 2
N_ODD = DEPTH // 2
ALPHA = (2 * DEPTH) ** 0.25
BETA = (8 * DEPTH) ** -0.25
LN_EPS = 1e-5
RMS_EPS = 1e-6
A_HEADS = 4
A_GROUP = 128
A_WIDTH = A_HEADS * A_GROUP
CHUNK = 128
POOL_WINDOWS = (2, 4, 8, 16)
B_GROUP = 128
B_WIDTH = len(POOL_WINDOWS) * B_GROUP
EVEN_IN = 2 * A_WIDTH + B_WIDTH
EVEN_MIX = A_WIDTH + B_WIDTH
W_C = D_MODEL // 2
C_HEADS = 8
C_BLOCK = W_C // C_HEADS
RG_CONV = 4
RG_C = 8.0
D_HEADS = 8
Q_LORA = 384
KV_LORA = 256
QK_NOPE = 64
QK_ROPE = 32
V_DIM = 64
ROPE_BASE = 10000.0
Q_BLOCK = 128
ATTN_SCALE = 1.0 / math.sqrt(QK_NOPE + QK_ROPE)
ODD_IN = 2 * W_C + Q_LORA + KV_LORA + QK_ROPE
ODD_MIX = W_C + D_HEADS * V_DIM
D_FF = 2816
FFN_CONV = 3

kernel_name = "hybrid_diffusion_prefix_trunk_step"

F32 = jnp.float32


def _standardize(x, eps):
    xf = x.astype(F32)
    mu = jnp.mean(xf, axis=-1, keepdims=True)
    var = jnp.mean(jnp.square(xf - mu), axis=-1, keepdims=True)
    return (xf - mu) * lax.rsqrt(var + eps)


def _layernorm(x, g, b):
    return (_standardize(x, LN_EPS) * g.astype(F32) + b.astype(F32)).astype(x.dtype)


def _rmsnorm(x, g):
    xf = x.astype(F32)
    y = xf * lax.rsqrt(jnp.mean(jnp.square(xf), axis=-1, keepdims=True) + RMS_EPS)
    return (y * g.astype(F32)).astype(x.dtype)


def _dwconv(x, w, b, pad_lo, pad_hi):
    C = x.shape[-1]
    y = lax.conv_general_dilated(
        x, w[:, None, :].astype(x.dtype), window_strides=(1,),
        padding=[(pad_lo, pad_hi)], dimension_numbers=('NWC', 'WIO', 'NWC'),
        feature_group_count=C)
    return y + b.astype(x.dtype)


def _axial_rope(rows):
    half = QK_ROPE // 2
    inv = ROPE_BASE ** (-jnp.arange(0, half, 2, dtype=F32) / half)
    r = jnp.repeat(jnp.arange(rows, dtype=F32), GRID_W)
    col = jnp.tile(jnp.arange(GRID_W, dtype=F32), rows)
    ang = jnp.concatenate([r[:, None] * inv, col[:, None] * inv], axis=-1)
    return jnp.cos(ang), jnp.sin(ang)


def _rope(x, cos, sin):
    xf = x.astype(F32)
    x1, x2 = xf[..., 0::2], xf[..., 1::2]
    out = jnp.stack([x1 * cos - x2 * sin, x1 * sin + x2 * cos], axis=-1)
    return out.reshape(x.shape).astype(x.dtype)


def _chunk_spatial_gate(u, v, w_s, b_s):
    B, L, _ = u.shape
    n = L // CHUNK
    vn = _standardize(v, LN_EPS).astype(v.dtype)
    vh = vn.reshape(B, n, CHUNK, A_HEADS, A_GROUP)
    s = jnp.einsum('hpq,bnqhc->bnphc', w_s, vh) + b_s.T[None, None, :, :, None]
    return u * s.reshape(B, L, A_WIDTH).astype(u.dtype)


def _multiscale_pool(p, w_pool, pool_scale):
    B, L, _ = p.shape
    pg = p.reshape(B, L, len(POOL_WINDOWS), B_GROUP)
    pgf = pg.astype(F32)
    cs = jnp.concatenate([jnp.zeros((B, 1, len(POOL_WINDOWS), B_GROUP), F32),
                          jnp.cumsum(pgf, axis=1)], axis=1)
    t = jnp.arange(L)
    outs = []
    for gi, w in enumerate(POOL_WINDOWS):
        lo = jnp.clip(t - w // 2, 0, L)
        hi = jnp.clip(t + w - w // 2, 0, L)
        csg = cs[:, :, gi]
        cnt = (hi - lo).astype(F32)[:, None]
        outs.append((csg[:, hi] - csg[:, lo]) / cnt - pgf[:, :, gi])
    pooled = jnp.stack(outs, axis=2).astype(p.dtype)
    y = jnp.einsum('blgc,gcd->blgd', pooled, w_pool).reshape(B, L, B_WIDTH)
    return y * pool_scale


def _even_mixer(h, w_in, w_s, b_s, w_pool, pool_scale, w_out):
    z = h @ w_in
    u = jax.nn.gelu(z[..., :A_WIDTH])
    v = jax.nn.gelu(z[..., A_WIDTH:2 * A_WIDTH])
    p = z[..., 2 * A_WIDTH:]
    y_a = _chunk_spatial_gate(u, v, w_s, b_s)
    y_b = _multiscale_pool(p, w_pool, pool_scale)
    return jnp.concatenate([y_a, y_b], axis=-1) @ w_out


def _rglru_coeffs(xc, w_a, b_a, w_x, b_x, lam):
    B, L, _ = xc.shape
    xh = xc.reshape(B, L, C_HEADS, C_BLOCK)
    r = jax.nn.sigmoid((jnp.einsum('blhi,hij->blhj', xh, w_a).reshape(B, L, W_C) + b_a).astype(F32))
    i = jax.nn.sigmoid((jnp.einsum('blhi,hij->blhj', xh, w_x).reshape(B, L, W_C) + b_x).astype(F32))
    log_a = -RG_C * r * jax.nn.softplus(-lam.astype(F32))
    a = jnp.exp(log_a)
    bx = jnp.sqrt(-jnp.expm1(2.0 * log_a)) * i * xc.astype(F32)
    return a, bx


def _linear_scan(a, bx, h0, reverse):
    def step(h, ab):
        h = ab[0] * h + ab[1]
        return h, h
    hT, hs = lax.scan(step, h0, (jnp.swapaxes(a, 0, 1), jnp.swapaxes(bx, 0, 1)), reverse=reverse)
    return jnp.swapaxes(hs, 0, 1), hT


def _rglru(xr, gr, conv_w, conv_b, w_a, b_a, w_x, b_x, lam, h0_fwd, h0_bwd):
    xc = _dwconv(xr, conv_w, conv_b, 1, 2)
    a_f, b_f = _rglru_coeffs(xc, w_a[0], b_a[0], w_x[0], b_x[0], lam[0])
    h_f, hT_f = _linear_scan(a_f, b_f, h0_fwd, False)
    a_b, b_b = _rglru_coeffs(xc, w_a[1], b_a[1], w_x[1], b_x[1], lam[1])
    h_b, hT_b = _linear_scan(a_b, b_b, h0_bwd, True)
    y = (h_f + h_b).astype(xr.dtype) * jax.nn.gelu(gr)
    return y, hT_f, hT_b


def _mla_queries(q_lat, q_g, w_uq):
    q = jnp.einsum('blr,rhe->blhe', _rmsnorm(q_lat, q_g), w_uq)
    return q[..., :QK_NOPE], q[..., QK_NOPE:]


def _mla_kv(ckv, w_uk, w_uv):
    return (jnp.einsum('blr,rhe->blhe', ckv, w_uk), jnp.einsum('blr,rhe->blhe', ckv, w_uv))


def _attend(q_nope, q_rope, k_nope, k_rope, v):
    B, Lq, H, _ = q_nope.shape
    nb = Lq // Q_BLOCK
    qn = jnp.moveaxis(q_nope.reshape(B, nb, Q_BLOCK, H, QK_NOPE), 1, 0)
    qr = jnp.moveaxis(q_rope.reshape(B, nb, Q_BLOCK, H, QK_ROPE), 1, 0)

    def block(args):
        qn_b, qr_b = args
        s = (jnp.einsum('bqhd,bkhd->bhqk', qn_b, k_nope, preferred_element_type=F32)
             + jnp.einsum('bqhd,bkd->bhqk', qr_b, k_rope, preferred_element_type=F32)) * ATTN_SCALE
        pr = jax.nn.softmax(s, axis=-1).astype(v.dtype)
        return jnp.einsum('bhqk,bkhd->bqhd', pr, v)

    o = lax.map(block, (qn, qr))
    return jnp.moveaxis(o, 0, 1).reshape(B, Lq, H * V_DIM)


def _odd_split(h, w_in):
    z = h @ w_in
    o1 = W_C
    o2 = 2 * W_C
    o3 = o2 + Q_LORA
    o4 = o3 + KV_LORA
    return z[..., :o1], z[..., o1:o2], z[..., o2:o3], z[..., o3:o4], z[..., o4:]


def _odd_context(h, w_in, conv_w, conv_b, w_a, b_a, w_x, b_x, lam, q_g, kv_g, w_uq, w_uk, w_uv, w_out):
    xr, gr, q_lat, kv_lat, kr = _odd_split(h, w_in)
    zero = jnp.zeros((h.shape[0], W_C), F32)
    y_c, hT_f, hT_b = _rglru(xr, gr, conv_w, conv_b, w_a, b_a, w_x, b_x, lam, zero, zero)
    ckv = _rmsnorm(kv_lat, kv_g)
    qn, qr = _mla_queries(q_lat, q_g, w_uq)
    kn, v = _mla_kv(ckv, w_uk, w_uv)
    y_d = _attend(qn, qr, kn, kr, v)
    out = jnp.concatenate([y_c, y_d], axis=-1) @ w_out
    return out, ckv, kr, jnp.stack([hT_f, hT_b], axis=1)


def _odd_latent(h, ctx_ckv, ctx_kr, ctx_state, cos, sin,
                w_in, conv_w, conv_b, w_a, b_a, w_x, b_x, lam, q_g, kv_g, w_uq, w_uk, w_uv, w_out):
    xr, gr, q_lat, kv_lat, kr = _odd_split(h, w_in)
    st = ctx_state.astype(F32)
    y_c, _, _ = _rglru(xr, gr, conv_w, conv_b, w_a, b_a, w_x, b_x, lam, st[:, 0], st[:, 1])
    ckv = _rmsnorm(kv_lat, kv_g)
    qn, qr = _mla_queries(q_lat, q_g, w_uq)
    qr = _rope(qr, cos[:, None, :], sin[:, None, :])
    kr = _rope(kr, cos, sin)
    kn_all, v_all = _mla_kv(jnp.concatenate([ctx_ckv.astype(ckv.dtype), ckv], axis=1), w_uk, w_uv)
    kr_all = jnp.concatenate([ctx_kr.astype(kr.dtype), kr], axis=1)
    y_d = _attend(qn, qr, kn_all, kr_all, v_all)
    return jnp.concatenate([y_c, y_d], axis=-1) @ w_out


def _conv_ffn(h, w_up, conv_w, conv_b, w_down):
    z = _dwconv(h @ w_up, conv_w, conv_b, 1, 1)
    g, v = z[..., :D_FF], z[..., D_FF:]
    return (jax.nn.silu(g) * v) @ w_down


def setup_inputs(seed: int = 0) -> dict:
    key = jax.random.key(seed)
    ks = iter(jax.random.split(key, 48))
    D = D_MODEL

    def nrm(shape, scale):
        return scale * jax.random.normal(next(ks), shape, F32)

    u = jax.random.uniform(next(ks), (N_ODD, 2, W_C), F32, 0.9, 0.999)
    a0 = u ** (1.0 / RG_C)
    rg_lam = jnp.log(a0) - jnp.log1p(-a0)
    return {
        'x_prompt': nrm((BATCH, SEQ, D), 1.0),
        'x_sample': nrm((DEC_BATCH, DEC_SEQ, D), 1.0),
        'cache_mla_ckv': nrm((DEC_BATCH, N_ODD, PAST_LEN, KV_LORA), 1.0),
        'cache_mla_krope': nrm((DEC_BATCH, N_ODD, PAST_LEN, QK_ROPE), 1.0),
        'state_rglru': nrm((DEC_BATCH, N_ODD, 2, W_C), 0.5),
        'c': nrm((DEC_BATCH, D), 1.0),
        'c_ctx': nrm((D,), 1.0),
        'w_mod': nrm((DEPTH, D, 6 * D), 0.5 * D ** -0.5),
        'b_mod': nrm((DEPTH, 6 * D), 0.02),
        'ln1_g': 1.0 + nrm((DEPTH, D), 0.02),
        'ln1_b': nrm((DEPTH, D), 0.02),
        'ln2_g': 1.0 + nrm((DEPTH, D), 0.02),
        'ln2_b': nrm((DEPTH, D), 0.02),
        'even_w_in': nrm((N_EVEN, D, EVEN_IN), D ** -0.5),
        'even_w_s': nrm((N_EVEN, A_HEADS, CHUNK, CHUNK), CHUNK ** -0.5),
        'even_b_s': 1.0 + nrm((N_EVEN, A_HEADS, CHUNK), 0.02),
        'even_w_pool': nrm((N_EVEN, len(POOL_WINDOWS), B_GROUP, B_GROUP), B_GROUP ** -0.5),
        'even_pool_scale': 1.0 + nrm((N_EVEN, B_WIDTH), 0.1),
        'even_w_out': nrm((N_EVEN, EVEN_MIX, D), BETA * EVEN_MIX ** -0.5),
        'odd_w_in': nrm((N_ODD, D, ODD_IN), D ** -0.5),
        'rg_conv_w': nrm((N_ODD, RG_CONV, W_C), RG_CONV ** -0.5),
        'rg_conv_b': nrm((N_ODD, W_C), 0.02),
        'rg_w_a': nrm((N_ODD, 2, C_HEADS, C_BLOCK, C_BLOCK), C_BLOCK ** -0.5),
        'rg_b_a': nrm((N_ODD, 2, W_C), 0.02),
        'rg_w_x': nrm((N_ODD, 2, C_HEADS, C_BLOCK, C_BLOCK), C_BLOCK ** -0.5),
        'rg_b_x': nrm((N_ODD, 2, W_C), 0.02),
        'rg_lam': rg_lam,
        'mla_q_g': 1.0 + nrm((N_ODD, Q_LORA), 0.02),
        'mla_kv_g': 1.0 + nrm((N_ODD, KV_LORA), 0.02),
        'mla_w_uq': nrm((N_ODD, Q_LORA, D_HEADS, QK_NOPE + QK_ROPE), Q_LORA ** -0.5),
        'mla_w_uk': nrm((N_ODD, KV_LORA, D_HEADS, QK_NOPE), KV_LORA ** -0.5),
        'mla_w_uv': nrm((N_ODD, KV_LORA, D_HEADS, V_DIM), KV_LORA ** -0.5),
        'odd_w_out': nrm((N_ODD, ODD_MIX, D), BETA * ODD_MIX ** -0.5),
        'ffn_w_up': nrm((DEPTH, D, 2 * D_FF), D ** -0.5),
        'ffn_conv_w': nrm((DEPTH, FFN_CONV, 2 * D_FF), FFN_CONV ** -0.5),
        'ffn_conv_b': nrm((DEPTH, 2 * D_FF), 0.02),
        'ffn_w_down': nrm((DEPTH, D_FF, D), BETA * D_FF ** -0.5),
    }


def reference(x_prompt, x_sample, cache_mla_ckv, cache_mla_krope, state_rglru, c, c_ctx,
              w_mod, b_mod, ln1_g, ln1_b, ln2_g, ln2_b,
              even_w_in, even_w_s, even_b_s, even_w_pool, even_pool_scale, even_w_out,
              odd_w_in, rg_conv_w, rg_conv_b, rg_w_a, rg_b_a, rg_w_x, rg_b_x, rg_lam,
              mla_q_g, mla_kv_g, mla_w_uq, mla_w_uk, mla_w_uv, odd_w_out,
              ffn_w_up, ffn_conv_w, ffn_conv_b, ffn_w_down):
    rows = x_sample.shape[1] // GRID_W
    cos_lat, sin_lat = _axial_rope(rows)
    s_ctx = jax.nn.silu(c_ctx)
    s_lat = jax.nn.silu(c)
    xp, xs = x_prompt, x_sample
    ckv_list, kr_list, st_list = [], [], []
    for l in range(DEPTH):
        sh1c, sc1c, g1c, sh2c, sc2c, g2c = jnp.split(s_ctx @ w_mod[l] + b_mod[l], 6, axis=-1)
        sh1s, sc1s, g1s, sh2s, sc2s, g2s = jnp.split((s_lat @ w_mod[l] + b_mod[l])[:, None, :], 6, axis=-1)
        hp = xp * (1 + sc1c) + sh1c
        hs = xs * (1 + sc1s) + sh1s
        if l % 2 == 0:
            e = l // 2
            dp = _even_mixer(hp, even_w_in[e], even_w_s[e], even_b_s[e], even_w_pool[e],
                             even_pool_scale[e], even_w_out[e])
            ds = _even_mixer(hs, even_w_in[e], even_w_s[e], even_b_s[e], even_w_pool[e],
                             even_pool_scale[e], even_w_out[e])
        else:
            o = l // 2
            dp, ckv, kr, st = _odd_context(
                hp, odd_w_in[o], rg_conv_w[o], rg_conv_b[o], rg_w_a[o], rg_b_a[o], rg_w_x[o], rg_b_x[o],
                rg_lam[o], mla_q_g[o], mla_kv_g[o], mla_w_uq[o], mla_w_uk[o], mla_w_uv[o], odd_w_out[o])
            ckv_list.append(ckv)
            kr_list.append(kr)
            st_list.append(st)
            ds = _odd_latent(
                hs, cache_mla_ckv[:, o], cache_mla_krope[:, o], state_rglru[:, o], cos_lat, sin_lat,
                odd_w_in[o], rg_conv_w[o], rg_conv_b[o], rg_w_a[o], rg_b_a[o], rg_w_x[o], rg_b_x[o],
                rg_lam[o], mla_q_g[o], mla_kv_g[o], mla_w_uq[o], mla_w_uk[o], mla_w_uv[o], odd_w_out[o])
        xp = _layernorm(ALPHA * xp + g1c * dp, ln1_g[l], ln1_b[l])
        xs = _layernorm(ALPHA * xs + g1s * ds, ln1_g[l], ln1_b[l])
        hp = xp * (1 + sc2c) + sh2c
        hs = xs * (1 + sc2s) + sh2s
        fp = _conv_ffn(hp, ffn_w_up[l], ffn_conv_w[l], ffn_conv_b[l], ffn_w_down[l])
        fs = _conv_ffn(hs, ffn_w_up[l], ffn_conv_w[l], ffn_conv_b[l], ffn_w_down[l])
        xp = _layernorm(ALPHA * xp + g2c * fp, ln2_g[l], ln2_b[l])
        xs = _layernorm(ALPHA * xs + g2s * fs, ln2_g[l], ln2_b[l])
    new_mla_ckv = jnp.stack(ckv_list, axis=1)
    new_mla_krope = jnp.stack(kr_list, axis=1)
    new_rglru_state = jnp.stack(st_list, axis=1)
    return (xp, xs, new_mla_ckv, new_mla_krope, new_rglru_state)
```

```python
import math
import numpy as np
import concourse.bass as bass
import concourse.mybir as mybir
from concourse.bass_utils import run_bass_kernel_spmd

F32 = mybir.dt.float32
BF16 = mybir.dt.bfloat16
U8 = mybir.dt.uint8
AF = mybir.ActivationFunctionType
ALU = mybir.AluOpType

D = 1024
NT = 1024
DEPTH = 2
ALPHA = (2 * DEPTH) ** 0.25
LN_EPS = 1e-5
RMS_EPS = 1e-6
D_FF = 2816
NJ = D_FF // 128
ATTN_SCALE = 1.0 / math.sqrt(96.0)
NEG = -512.0
GELU = AF.Gelu_apprx_tanh

V_BMOD = 0
V_LN = 96
V_PSC = 160
V_RGW = 164
V_RGB = 180
V_BA = 184
V_BX = 192
V_LAM = 200
V_QG = 208
V_KVG = 211
V_FCW = 216
V_FCB = 480
V_ST = 568
V_MASK = 576
NV = 580


class Tile:
    __slots__ = ("name", "ap", "p0", "p1", "b0", "b1", "w", "r", "live")


class Sched:
    def __init__(self, nc, arena, arena_bytes, n_dma_sems=56):
        self.nc = nc
        self.arena = arena
        self.arena_bytes = arena_bytes
        self.E = {"pe": nc.tensor, "act": nc.scalar, "dve": nc.vector, "pool": nc.gpsimd, "sp": nc.sync}
        self.sem = {k: nc.alloc_semaphore("sem_" + k) for k in ("pe", "act", "dve", "pool")}
        self.cnt = {k: 0 for k in self.sem}
        self.seen = {k: {} for k in self.E}
        self.dsem = [nc.alloc_semaphore("dsem%d" % i) for i in range(n_dma_sems)]
        self.dcnt = [0] * n_dma_sems
        self.ring = {"pool": list(range(0, 36)), "sp": list(range(36, n_dma_sems))}
        self.rpos = {"pool": 0, "sp": 0}
        self.tiles = []
        self.psum_tiles = []
        self.nwaits = 0
        self.pool_dmas = []
        self.MAX_SWDGE = 4

    def tile(self, name, off, cols, dtype, p0=0, p1=128):
        dsz = 4 if dtype == F32 else 2
        nbytes = cols * dsz
        assert off % 4 == 0 and off + nbytes <= self.arena_bytes, (name, off, nbytes)
        t = Tile()
        t.name = name
        t.ap = self.arena[p0:p1, off:off + nbytes].bitcast(dtype)
        t.p0, t.p1, t.b0, t.b1 = p0, p1, off, off + nbytes
        t.w = None
        t.r = {}
        t.live = True
        pend = []
        for o in self.tiles:
            if o.b0 < t.b1 and t.b0 < o.b1 and o.p0 < t.p1 and t.p0 < o.p1:
                o.live = False
                if o.w is not None:
                    pend.append(o.w)
                pend.extend(o.r.values())
        self.tiles.append(t)
        best = {}
        for ev in pend:
            if ev[2] > best.get(ev[1], (None, None, 0))[2]:
                best[ev[1]] = ev
        for k, ev in best.items():
            t.r[("inh", k)] = ev
        return t

    def psum(self, name, ap):
        t = Tile()
        t.name = name
        t.ap = ap
        t.w = None
        t.r = {}
        t.live = True
        t.p0, t.p1, t.b0, t.b1 = 0, 128, -1, -1
        return t

    def _wait(self, eng, evs):
        best = {}
        for (sem, key, val) in evs:
            if val > best.get(key, (None, 0))[1]:
                best[key] = (sem, val)
        for key, (sem, val) in best.items():
            if eng == "pe" and key == "pe":
                continue
            if self.seen[eng].get(key, 0) < val:
                self.E[eng].wait_ge(sem, val)
                self.seen[eng][key] = val
                self.nwaits += 1

    def _collect(self, r, w, eng=None):
        evs = []
        for t in r:
            assert t.live, "read of retired tile " + t.name
            if t.w is not None:
                evs.append(t.w)
            if t.b0 == -1:
                evs.extend(ev for k, ev in t.r.items() if k != eng)
        for t in w:
            assert t.live, "write of retired tile " + t.name
            if t.w is not None:
                evs.append(t.w)
            evs.extend(t.r.values())
        return evs

    def _commit(self, key, ev, r, w):
        for t in r:
            t.r[key] = ev
        for t in w:
            t.w = ev
            t.r = {}

    def op(self, eng, fn, r=(), w=(), pe_fence=None):
        self._wait(eng, self._collect(r, w, eng))
        inst = fn()
        if pe_fence is not None:
            inst = self.nc.tensor.matmul(pe_fence[0], pe_fence[1], pe_fence[1], start=True, stop=True)
        self.cnt[eng] += 1
        inst.then_inc(self.sem[eng], 1)
        ev = (self.sem[eng], eng, self.cnt[eng])
        self._commit(eng, ev, r, w)
        return ev

    def dma(self, q, out_ap, in_ap, r=(), w=()):
        evs = self._collect(r, w)
        ring = self.ring[q]
        i = ring[self.rpos[q] % len(ring)]
        self.rpos[q] += 1
        key = "d%d" % i
        if self.dcnt[i] > 0:
            evs.append((self.dsem[i], key, self.dcnt[i]))
        if q == "pool":
            if len(self.pool_dmas) >= self.MAX_SWDGE:
                evs.append(self.pool_dmas[-self.MAX_SWDGE])
        self._wait(q, evs)
        inst = self.E[q].dma_start(out=out_ap, in_=in_ap)
        self.dcnt[i] += 16
        inst.then_inc(self.dsem[i], 16)
        ev = (self.dsem[i], key, self.dcnt[i])
        if q == "pool":
            self.pool_dmas.append(ev)
        self._commit(key, ev, r, w)
        return ev

    def fence(self, eng, tiles, scratch):
        if eng == "act":
            fn = lambda: self.nc.scalar.activation(out=scratch.ap[:, 0:1], in_=scratch.ap[:, 1:2],
                                                   func=AF.Identity)
        elif eng == "dve":
            fn = lambda: self.nc.vector.tensor_copy(out=scratch.ap[:, 0:1], in_=scratch.ap[:, 1:2])
        else:
            fn = lambda: self.nc.gpsimd.tensor_copy(out=scratch.ap[:, 0:1], in_=scratch.ap[:, 1:2])
        inst = fn()
        self.cnt[eng] += 1
        inst.then_inc(self.sem[eng], 1)
        ev = (self.sem[eng], eng, self.cnt[eng])
        for t in tiles:
            t.w = ev

    def wait_all(self, eng, evs):
        self._wait(eng, evs)


def build_program(stop_after=None, debug=False, skip=()):
    nc = bass.Bass("TRN2", target_bir_lowering=False)

    def din(name, shape):
        return nc.dram_tensor(name, list(shape), F32, kind="ExternalInput").ap()

    def dout(name, shape):
        return nc.dram_tensor(name, list(shape), F32, kind="ExternalOutput").ap()

    d_xT = din("xT", [D, NT])
    d_cv = din("cv", [128, 8])
    d_vecs = din("vecs", [128, NV])
    d_wmod = din("w_mod", [2, D, 6 * D])
    d_ewin = din("even_w_in", [D, 1536])
    d_ewout = din("even_w_out", [D, D])
    d_wsT = din("w_sT", [128, 512])
    d_bs = din("b_s", [1, 512])
    d_wpool = din("w_pool", [128, 512])
    d_invcnt = din("invcnt", [128, 4096])
    d_owin_a = din("odd_w_in_a", [D, 1024])
    d_owin_b = din("odd_w_in_b", [D, 640])
    d_wkr = din("w_kr_pad", [D, 96])
    d_wkrs = din("w_kr_swap", [D, 96])
    d_wgate = din("w_gate", [128, 2048])
    d_wuq = din("w_uq", [384, 768])
    d_wuqs = din("w_uq_swap", [384, 768])
    d_wuk = din("w_uk", [256, 512])
    d_wuv = din("w_uv", [256, 512])
    d_owout = din("odd_w_out", [D, D])
    d_ckvc = din("ckv_cacheT", [256, 512])
    d_krc = din("kr_cacheT", [32, 512])
    d_maskq = din("maskq", [4, NT])
    d_maskk = din("maskk", [4, 1536])
    d_ropeC = din("ropeC", [32, NT])
    d_ropeS = din("ropeS", [32, NT])
    d_wup = din("ffn_w_up_r", [2, 6, D, 1024])
    d_wdown = din("ffn_w_down", [2, D_FF, D])
    o_yT = dout("yT", [D, NT])
    o_ckvT = dout("ckvT", [256, NT])
    o_krT = dout("krT", [32, NT])
    o_st = dout("st", [128, 32])
    o_dbg = dout("dbg", [128, 12288]) if debug else None
    o_dump = dout("dump", [16, 128, 1024]) if debug else None
    dump_state = {"n": 0, "names": []}

    ARENA = 206 * 1024
    arena = nc.alloc_sbuf_tensor("arena", [128, ARENA], U8)
    S = Sched(nc, arena, ARENA)
    T = S.tile
    out_events = []

    psum_all = nc.alloc_psum_tensor("psum_all", [128, 4096], F32)
    PB = [S.psum("pb%d" % i, psum_all[:, i * 512:(i + 1) * 512]) for i in range(8)]

    def pspan(i, n=2):
        return psum_all[:, i * 512:(i + n) * 512]

    off = 0
    xT = []
    for c in range(8):
        xT.append(T("xT%d" % c, off, NT, F32)); off += NT * 4
    hT = []
    for c in range(8):
        hT.append(T("hT%d" % c, off, NT, BF16)); off += NT * 2
    vecs = T("vecs", off, NV, F32); off += NV * 4
    cvt = T("cv", off, 8, F32); off += 32
    s_bf = T("s_bf", off, 8, BF16); off += 32
    modT = [T("modT0", off, 48, F32), T("modT1", off + 192, 48, F32)]; off += 384
    small = [T("small%d" % l, off + l * 256, 64, F32) for l in range(2)]; off += 512
    ones_bf = T("ones_bf", off, 128, BF16); off += 256
    epsln = T("epsln", off, 1, F32); off += 4
    epsrms = T("epsrms", off, 1, F32); off += 4
    nkeep = T("nkeep", off, 1, F32); off += 4
    off = (off + 63) // 64 * 64
    fixv = T("fixv", off, 2 * 2 * 44 + 16, F32); off += (2 * 2 * 44 + 16) * 4
    off = (off + 63) // 64 * 64
    fsc = {e: T("fsc_" + e, off + i * 8, 2, F32) for i, e in enumerate(("act", "dve", "pool"))}; off += 64
    for e_ in fsc.values():
        pass
    zo = T("zo", off, 128, BF16); off += 256
    one_f = T("one_f", off, 1, F32); off += 64
    st_t = T("st_t", off, 32, F32); off += 128
    clam = T("clam", off, 16, F32); off += 64
    off = (off + 63) // 64 * 64
    PERSIST_END = off

    v = vecs.ap

    def VC(col, n=1):
        return v[:, col:col + n]

    def dump(name, tile_, ap=None, p0=0, p1=128):
        if not debug:
            return
        i = dump_state["n"]
        dump_state["n"] += 1
        dump_state["names"].append(name)
        stg = T("stg%d" % i, ARENA - 4608, NT, F32)
        src = tile_.ap if ap is None else ap
        ncol = src.shape[-1]
        S.op("dve", lambda: nc.vector.tensor_copy(out=stg.ap[p0:p1, 0:ncol], in_=src), r=[tile_], w=[stg])
        out_events.append(S.dma("sp", o_dump[i, p0:p1, 0:ncol], stg.ap[p0:p1, 0:ncol], r=[stg]))
    build_program.dump_names = dump_state["names"]

    for c in range(8):
        S.dma("sp", xT[c].ap, d_xT[c * 128:(c + 1) * 128, :], w=[xT[c]])
    S.dma("sp", vecs.ap, d_vecs, w=[vecs])
    S.dma("sp", cvt.ap, d_cv, w=[cvt])
    S.op("pool", lambda: nc.gpsimd.memset(ones_bf.ap, 1.0), w=[ones_bf])
    for e_ in ("act", "dve", "pool"):
        S.op("pool", lambda: nc.gpsimd.memset(fsc[e_].ap, 0.0), w=[fsc[e_]])
    S.op("pool", lambda: nc.gpsimd.memset(one_f.ap, 1.0), w=[one_f])
    S.op("pool", lambda: nc.gpsimd.memset(zo.ap[:, 0:64], 0.0), w=[zo])
    S.op("pool", lambda: nc.gpsimd.memset(zo.ap[:, 64:128], 1.0), w=[zo])
    S.op("pool", lambda: nc.gpsimd.memset(epsln.ap, LN_EPS), w=[epsln])
    S.op("pool", lambda: nc.gpsimd.memset(epsrms.ap, RMS_EPS), w=[epsrms])
    S.op("dve", lambda: nc.vector.tensor_scalar(out=nkeep.ap, in0=VC(V_MASK), scalar1=-1.0, scalar2=None,
                                                 op0=ALU.add), r=[vecs], w=[nkeep])
    for l in range(2):
        for ti, tap in enumerate((0, 2)):
            S.op("dve", lambda l=l, ti=ti, tap=tap: nc.vector.tensor_scalar(
                out=fixv.ap[:, (l * 2 + ti) * 44:(l * 2 + ti + 1) * 44],
                in0=VC(V_FCW + (l * 3 + tap) * 44, 44), scalar1=nkeep.ap[:, 0:1], scalar2=None, op0=ALU.mult),
                r=[vecs, nkeep], w=[fixv])
    S.op("dve", lambda: nc.vector.tensor_scalar(out=fixv.ap[:, 176:192], in0=VC(V_RGW, 16),
                                                 scalar1=nkeep.ap[:, 0:1], scalar2=None, op0=ALU.mult),
         r=[vecs, nkeep], w=[fixv])
    S.op("act", lambda: nc.scalar.activation(out=s_bf.ap, in_=cvt.ap, func=AF.Silu), r=[cvt], w=[s_bf])

    def mod_setup(l, buf_off, W=512):
        assert W == 512
        npiece = 6144 // W
        nch = W // 128
        bufs = [T("wm%d_%d" % (l, i), buf_off + i * W * 16, W * 8, BF16) for i in range(2)]
        rowt = [T("wmrow%d_%d" % (l, i), buf_off + 2 * W * 16 + i * W * 4, W, F32, 0, 1) for i in range(2)]
        ps = PB[6]
        prow = PB[7]

        def load(k):
            b = bufs[k % 2]
            S.dma("pool", b.ap.rearrange("p (c n) -> p c n", c=8),
                  d_wmod[l, :, k * W:(k + 1) * W].rearrange("(c p) n -> p c n", p=128), w=[b])

        def transposes(k):
            rt = rowt[k % 2]

            def mm():
                last = None
                for n in range(nch):
                    col = k * nch + n
                    last = nc.tensor.matmul(ps.ap[:, col:col + 1], rt.ap[0:1, n * 128:(n + 1) * 128],
                                            one_f.ap[0:1, 0:1], start=True, stop=True)
                return last
            S.op("pe", mm, r=[rt, one_f], w=[ps])

        def piece(k):
            if k + 1 < npiece:
                load(k + 1)
            b = bufs[k % 2]
            bv = b.ap.rearrange("p (c n) -> p c n", c=8)

            def mm():
                last = None
                for kc in range(8):
                    last = nc.tensor.matmul(prow.ap[0:1, 0:W], s_bf.ap[:, kc:kc + 1], bv[:, kc, :],
                                            start=(kc == 0), stop=(kc == 7))
                return last
            S.op("pe", mm, r=[b, s_bf], w=[prow])
            rt = rowt[k % 2]
            S.op("dve", lambda: nc.vector.tensor_copy(out=rt.ap, in_=prow.ap[0:1, 0:W]), r=[prow], w=[rt])
            flush()
            pend_tr.append(k)
            if k == npiece - 1:
                flush()

        pend_tr = []

        def flush():
            while pend_tr:
                transposes(pend_tr.pop(0))

        def finish_a():
            flush()
            m = modT[l].ap
            sm = small[l].ap
            S.op("dve", lambda: nc.vector.tensor_tensor(out=m[:, 0:16], in0=ps.ap[:, 0:16],
                                                         in1=VC(V_BMOD + l * 48, 16), op=ALU.add),
                 r=[ps, vecs], w=[modT[l]])
            S.op("dve", lambda: nc.vector.tensor_scalar(out=sm[:, 0:8], in0=m[:, 8:16], scalar1=1.0, scalar2=None,
                                                         op0=ALU.add), r=[modT[l]], w=[small[l]])

        def finish_b():
            flush()
            m = modT[l].ap
            sm = small[l].ap
            S.op("dve", lambda: nc.vector.tensor_tensor(out=m[:, 16:48], in0=ps.ap[:, 16:48],
                                                         in1=VC(V_BMOD + l * 48 + 16, 32), op=ALU.add),
                 r=[ps, vecs], w=[modT[l]])
            S.op("dve", lambda: nc.vector.tensor_scalar(out=sm[:, 8:16], in0=m[:, 32:40], scalar1=1.0, scalar2=None,
                                                         op0=ALU.add), r=[modT[l]], w=[small[l]])
            S.op("dve", lambda: nc.vector.tensor_scalar(out=sm[:, 32:40], in0=VC(V_LN + l * 32 + 8, 8),
                                                         scalar1=ALPHA, scalar2=None, op0=ALU.mult),
                 r=[vecs], w=[small[l]])
            S.op("dve", lambda: nc.vector.tensor_scalar(out=sm[:, 40:48], in0=VC(V_LN + l * 32 + 24, 8),
                                                         scalar1=ALPHA, scalar2=None, op0=ALU.mult),
                 r=[vecs], w=[small[l]])
            S.op("dve", lambda: nc.vector.tensor_tensor(out=sm[:, 16:24], in0=VC(V_LN + l * 32 + 8, 8),
                                                         in1=sm[:, 8:16], op=ALU.mult),
                 r=[vecs, small[l]], w=[small[l]])
            S.op("dve", lambda: nc.vector.tensor_tensor(out=sm[:, 16:24], in0=sm[:, 16:24], in1=m[:, 24:32],
                                                         op=ALU.add), r=[modT[l], small[l]], w=[small[l]])

        def finish():
            finish_a()
            finish_b()
        finish.a = finish_a
        finish.b = finish_b
        load(0)
        return [lambda k=k: piece(k) for k in range(npiece)], finish

    def modulation(l, buf_off):
        pieces, finish = mod_setup(l, buf_off, 512)
        for p_ in pieces:
            p_()
        finish()

    def hb_next(l):
        sm = small[l].ap
        sn = small[l + 1].ap
        S.op("dve", lambda: nc.vector.tensor_tensor(out=sm[:, 24:32], in0=VC(V_LN + l * 32 + 24, 8), in1=sn[:, 0:8],
                                                     op=ALU.mult), r=[vecs, small[l], small[l + 1]], w=[small[l]])
        S.op("dve", lambda: nc.vector.tensor_tensor(out=sm[:, 24:32], in0=sm[:, 24:32], in1=modT[l + 1].ap[:, 0:8],
                                                     op=ALU.add), r=[modT[l + 1], small[l]], w=[small[l]])

    MIX_BASE = PERSIST_END
    mod0_pieces, mod0_finish = mod_setup(0, MIX_BASE + 129088, 512)
    for p_ in mod0_pieces[:4]:
        p_()
    mod0_finish.a()

    for c in range(8):
        S.op("act", lambda c=c: nc.scalar.activation(out=hT[c].ap, in_=xT[c].ap, func=AF.Identity,
                                                     bias=modT[0].ap[:, c:c + 1], scale=small[0].ap[:, c:c + 1]),
             r=[xT[c], modT[0], small[0]], w=[hT[c]])
        S.op("dve", lambda c=c: nc.vector.tensor_scalar(out=xT[c].ap, in0=xT[c].ap, scalar1=ALPHA, scalar2=None,
                                                         op0=ALU.mult), r=[xT[c]], w=[xT[c]])

    def proj_fm(w_view, ncols0, hsrc, psb, kch=8):
        def mm():
            last = None
            for th in range(2):
                for kc in range(kch):
                    last = nc.tensor.matmul(PB[psb + th].ap, w_view[:, kc, ncols0:ncols0 + 128],
                                            hsrc[kc].ap[:, th * 512:(th + 1) * 512],
                                            start=(kc == 0), stop=(kc == kch - 1))
            return last
        return mm

    def proj_fine(w_view, w_tile, ncols0, hsrc, psb, kch=8):
        for kc in range(kch):
            def mm(kc=kc):
                last = None
                for th in range(2):
                    last = nc.tensor.matmul(PB[psb + th].ap, w_view[:, kc, ncols0:ncols0 + 128],
                                            hsrc[kc].ap[:, th * 512:(th + 1) * 512],
                                            start=(kc == 0), stop=(kc == kch - 1))
                return last
            S.op("pe", mm, r=[w_tile, hsrc[kc]], w=[PB[psb], PB[psb + 1]])

    def layernorm(l, which, base, final=False):
        g_col = V_LN + l * 32 + (0 if which == 1 else 16)
        b_col = g_col + 8
        o = base
        ybf = [T("ybf%d" % i, o + i * 2048, NT, BF16) for i in range(2)]; o += 4096
        ysq = [T("ysq%d" % i, o + i * 2048, NT, BF16) for i in range(2)]; o += 4096
        msq = T("msq", o, NT, F32); o += 4096
        rstd = T("rstd", o, NT, F32); o += 4096
        nmr = T("nmr", o, NT, F32); o += 4096
        for c in range(8):
            yb, ys = ybf[c % 2], ysq[c % 2]
            S.op("dve", lambda c=c, yb=yb: nc.vector.tensor_copy(out=yb.ap, in_=xT[c].ap), r=[xT[c]], w=[yb])
            S.op("act", lambda c=c, ys=ys: nc.scalar.activation(out=ys.ap, in_=xT[c].ap, func=AF.Square),
                 r=[xT[c]], w=[ys])

            def mm(c=c, yb=yb, ys=ys):
                last = None
                for th in range(2):
                    nc.tensor.matmul(PB[th].ap, ones_bf.ap, yb.ap[:, th * 512:(th + 1) * 512],
                                     start=(c == 0), stop=(c == 7))
                    last = nc.tensor.matmul(PB[2 + th].ap, ones_bf.ap, ys.ap[:, th * 512:(th + 1) * 512],
                                            start=(c == 0), stop=(c == 7))
                return last
            S.op("pe", mm, r=[yb, ys, ones_bf], w=[PB[0], PB[1], PB[2], PB[3]])
        S1 = pspan(0)
        S2 = pspan(2)
        S.op("act", lambda: nc.scalar.activation(out=msq.ap, in_=S1, func=AF.Square, scale=1.0 / D),
             r=[PB[0], PB[1]], w=[msq])
        S.op("dve", lambda: nc.vector.scalar_tensor_tensor(out=rstd.ap, in0=S2, scalar=1.0 / D, in1=msq.ap,
                                                            op0=ALU.mult, op1=ALU.subtract),
             r=[PB[2], PB[3], msq], w=[rstd])
        S.op("act", lambda: nc.scalar.activation(out=rstd.ap, in_=rstd.ap, func=AF.Sqrt, bias=epsln.ap[:, 0:1]),
             r=[rstd, epsln], w=[rstd])
        S.op("dve", lambda: nc.vector.reciprocal(out=rstd.ap, in_=rstd.ap), r=[rstd], w=[rstd])
        S.op("dve", lambda: nc.vector.scalar_tensor_tensor(out=nmr.ap, in0=S1, scalar=-1.0 / D, in1=rstd.ap,
                                                            op0=ALU.mult, op1=ALU.mult),
             r=[PB[0], PB[1], rstd], w=[nmr])
        sm = small[l].ap
        for c in range(8):
            S.op("dve", lambda c=c: nc.vector.scalar_tensor_tensor(out=xT[c].ap, in0=xT[c].ap,
                                                                    scalar=VC(g_col + c), in1=rstd.ap,
                                                                    op0=ALU.mult, op1=ALU.mult),
                 r=[xT[c], vecs, rstd], w=[xT[c]])
            S.op("dve", lambda c=c: nc.vector.scalar_tensor_tensor(out=xT[c].ap, in0=nmr.ap,
                                                                    scalar=VC(g_col + c), in1=xT[c].ap,
                                                                    op0=ALU.mult, op1=ALU.add),
                 r=[xT[c], vecs, nmr], w=[xT[c]])
            if final:
                S.op("act", lambda c=c: nc.scalar.activation(out=xT[c].ap, in_=xT[c].ap, func=AF.Identity,
                                                             bias=VC(b_col + c)),
                     r=[xT[c], vecs], w=[xT[c]])
                out_events.append(S.dma("sp", o_yT[c * 128:(c + 1) * 128, :], xT[c].ap, r=[xT[c]]))
            else:
                if which == 1:
                    sc = sm[:, 8 + c:9 + c]; hb = sm[:, 16 + c:17 + c]; ab = sm[:, 32 + c:33 + c]
                else:
                    sc = small[l + 1].ap[:, c:c + 1]; hb = sm[:, 24 + c:25 + c]; ab = sm[:, 40 + c:41 + c]
                rr = [xT[c], small[l]] + ([small[l + 1]] if which == 2 else [])
                S.op("act", lambda c=c, sc=sc, hb=hb: nc.scalar.activation(out=hT[c].ap, in_=xT[c].ap,
                                                                            func=AF.Identity, bias=hb, scale=sc),
                     r=rr, w=[hT[c]])
                S.op("pool", lambda c=c, ab=ab: nc.gpsimd.tensor_scalar(out=xT[c].ap, in0=xT[c].ap, scalar1=ALPHA,
                                                                         scalar2=ab, op0=ALU.mult, op1=ALU.add),
                     r=[xT[c], small[l]], w=[xT[c]])

    def residual_evac(l, gate_col0, nch, psb):
        S.op("dve", lambda: nc.vector.scalar_tensor_tensor(out=xT[nch].ap, in0=pspan(psb),
                                                            scalar=modT[l].ap[:, gate_col0 + nch:gate_col0 + nch + 1],
                                                            in1=xT[nch].ap, op0=ALU.mult, op1=ALU.add),
             r=[PB[psb], PB[psb + 1], modT[l], xT[nch]], w=[xT[nch]])

    def even_mixer(l, base, hook_setup=None, pre_hooks=None, pre_finish=None):
        o = base
        w_inA = T("ew_inA", o, 8192, BF16); o += 16384
        w_inB = T("ew_inB", o, 4096, BF16); o += 8192
        w_out = T("ew_out", o, 8192, BF16); o += 16384
        wsT = T("wsT", o, 512, BF16); o += 1024
        wpool = T("wpool", o, 512, BF16); o += 1024
        bsf = T("bsf", o, 512, F32, 0, 1); o += 2048
        bshi = T("bshi", o, 512, BF16, 0, 1); o += 1024
        bslo = T("bslo", o, 512, BF16, 0, 1); o += 1024
        bstmp = T("bstmp", o, 512, F32, 0, 1); o += 2048
        invc = T("invcnt", o, 4096, F32); o += 16384
        uT = [T("uT%d" % i, o + i * 4096, NT, F32) for i in range(4)]; o += 16384
        vn = [T("vn%d" % i, o + i * 1024, 512, BF16) for i in range(8)]; o += 8192
        gv = [T("gv%d" % i, o + i * 2048, 512, F32) for i in range(2)]; o += 4096
        st6 = [T("st6_%d" % i, o + i * 32, 8, F32) for i in range(2)]; o += 64
        PW = 4 * 272
        Ph = [T("Ph%d" % i, o + i * PW * 4, PW, F32) for i in range(4)]; o += 4 * PW * 4
        Sa = [T("Sa%d" % i, o + i * PW * 4, PW, F32) for i in range(2)]; o += 2 * PW * 4
        Sb = [T("Sb%d" % i, o + i * PW * 4, PW, F32) for i in range(2)]; o += 2 * PW * 4
        o_pooled = o; o += 8192
        o_yT = o; o += 16384
        END = o
        pre = list(pre_hooks) if pre_hooks else []

        def run_pre():
            if pre:
                pre.pop(0)()

        S.dma("pool", w_inA.ap.rearrange("p (c n) -> p c n", c=8),
              d_ewin[:, 0:1024].rearrange("(c p) n -> p c n", p=128), w=[w_inA])
        S.dma("pool", w_inB.ap.rearrange("p (c n) -> p c n", c=8),
              d_ewin[:, 1024:1536].rearrange("(c p) n -> p c n", p=128), w=[w_inB])
        S.dma("pool", wsT.ap, d_wsT, w=[wsT])
        S.dma("pool", wpool.ap, d_wpool, w=[wpool])
        S.dma("sp", bsf.ap, d_bs, w=[bsf])
        S.dma("sp", invc.ap, d_invcnt, w=[invc])
        S.dma("pool", w_out.ap.rearrange("p (c n) -> p c n", c=8),
              d_ewout.rearrange("(c p) n -> p c n", p=128), w=[w_out])
        S.op("dve", lambda: nc.vector.tensor_copy(out=bshi.ap, in_=bsf.ap), r=[bsf], w=[bshi])
        S.op("dve", lambda: nc.vector.tensor_copy(out=bstmp.ap, in_=bshi.ap), r=[bshi], w=[bstmp])
        S.op("dve", lambda: nc.vector.tensor_tensor(out=bslo.ap, in0=bsf.ap, in1=bstmp.ap, op=ALU.subtract),
             r=[bsf, bstmp], w=[bslo])
        for t_ in Ph:
            S.op("pool", lambda t_=t_: nc.gpsimd.memset(t_.ap, 0.0), w=[t_])

        wA = w_inA.ap.rearrange("p (c n) -> p c n", c=8)
        wB = w_inB.ap.rearrange("p (c n) -> p c n", c=8)
        wO = w_out.ap.rearrange("p (c n) -> p c n", c=8)
        for n in range(4):
            psb = (n % 2) * 2
            S.op("pe", proj_fm(wA, n * 128, hT, psb), r=[w_inA] + hT, w=[PB[psb], PB[psb + 1]])
            S.op("act", lambda n=n, psb=psb: nc.scalar.activation(out=uT[n].ap, in_=pspan(psb), func=GELU),
                 r=[PB[psb], PB[psb + 1]], w=[uT[n]])
            run_pre()
        for g in range(4):
            psb = 4
            S.op("pe", proj_fm(wB, g * 128, hT, psb), r=[w_inB] + hT, w=[PB[psb], PB[psb + 1]])
            dst = Ph[g].ap.rearrange("p (s x) -> p s x", s=4)[:, :, 8:264]
            S.op("act", lambda g=g, psb=psb, dst=dst: nc.scalar.activation(
                out=dst, in_=pspan(psb).rearrange("p (s x) -> p s x", s=4), func=AF.Identity),
                r=[PB[psb], PB[psb + 1]], w=[Ph[g]])
            run_pre()
        for tc in range(8):
            psb = tc % 4
            def mmv(tc=tc, psb=psb):
                last = None
                for kc in range(8):
                    last = nc.tensor.matmul(PB[psb].ap, hT[kc].ap[:, tc * 128:(tc + 1) * 128], wA[:, kc, 512:1024],
                                            start=(kc == 0), stop=(kc == 7))
                return last
            S.op("pe", mmv, r=[w_inA] + hT, w=[PB[psb]])
            g_ = gv[tc % 2]
            s6 = st6[tc % 2]
            S.op("act", lambda g_=g_, psb=psb: nc.scalar.activation(out=g_.ap, in_=PB[psb].ap, func=GELU),
                 r=[PB[psb]], w=[g_])
            S.op("dve", lambda g_=g_, s6=s6: nc.vector.bn_stats(out=s6.ap[:, 0:6], in_=g_.ap), r=[g_], w=[s6])
            S.op("dve", lambda s6=s6: nc.vector.bn_aggr(out=s6.ap[:, 6:8], in_=s6.ap[:, 0:6]), r=[s6], w=[s6])
            S.op("act", lambda s6=s6: nc.scalar.activation(out=s6.ap[:, 7:8], in_=s6.ap[:, 7:8], func=AF.Sqrt,
                                                           bias=epsln.ap[:, 0:1]), r=[s6, epsln], w=[s6])
            S.op("dve", lambda s6=s6: nc.vector.reciprocal(out=s6.ap[:, 7:8], in_=s6.ap[:, 7:8]), r=[s6], w=[s6])
            S.op("dve", lambda g_=g_, s6=s6, tc=tc: nc.vector.tensor_scalar(
                out=vn[tc].ap, in0=g_.ap, scalar1=s6.ap[:, 6:7], scalar2=s6.ap[:, 7:8],
                op0=ALU.subtract, op1=ALU.mult), r=[g_, s6], w=[vn[tc]])
            run_pre()
        while pre:
            run_pre()
        if pre_finish is not None:
            pre_finish()
        pooled = [T("pooled%d" % i, o_pooled + i * 2048, NT, BF16) for i in range(4)]
        yT = [T("yT%d" % i, o_yT + i * 2048, NT, BF16) for i in range(8)]
        hooks, hook_finish = ([], None)
        if hook_setup is not None:
            hooks, hook_finish = hook_setup(w_inA.b0)

        def run_hook():
            if hooks:
                hooks.pop(0)()
        for h in range(4):
            psb = (h % 2) * 2
            def mmg(h=h, psb=psb):
                last = None
                for n in range(8):
                    o_ = PB[psb + n // 4].ap[:, (n % 4) * 128:(n % 4 + 1) * 128]
                    nc.tensor.matmul(o_, vn[n].ap[:, h * 128:(h + 1) * 128], wsT.ap[:, h * 128:(h + 1) * 128],
                                     start=True, stop=False)
                    nc.tensor.matmul(o_, ones_bf.ap[0:1, :], bshi.ap[0:1, h * 128:(h + 1) * 128],
                                     start=False, stop=False)
                    last = nc.tensor.matmul(o_, ones_bf.ap[0:1, :], bslo.ap[0:1, h * 128:(h + 1) * 128],
                                            start=False, stop=True)
                return last
            S.op("pe", mmg, r=vn + [wsT, ones_bf, bshi, bslo], w=[PB[psb], PB[psb + 1]])
            S.op("dve", lambda h=h, psb=psb: nc.vector.tensor_tensor(out=yT[h].ap, in0=pspan(psb), in1=uT[h].ap,
                                                                      op=ALU.mult),
                 r=[PB[psb], PB[psb + 1], uT[h]], w=[yT[h]])
            run_hook()
        for g in range(4):
            w_ = (2, 4, 8, 16)[g]
            P3 = Ph[g].ap.rearrange("p (s x) -> p s x", s=4)
            eng = "pool" if g % 2 == 0 else "dve"
            EN = S.E[eng]
            S.op(eng, lambda P3=P3, EN=EN: EN.tensor_scalar(out=P3[:, 1:4, 0:8], in0=P3[:, 0:3, 256:264],
                                                            scalar1=VC(V_MASK), scalar2=None, op0=ALU.mult),
                 r=[Ph[g], vecs], w=[Ph[g]])
            S.op(eng, lambda P3=P3, EN=EN: EN.tensor_scalar(out=P3[:, 0:3, 264:272], in0=P3[:, 1:4, 8:16],
                                                            scalar1=VC(V_MASK), scalar2=None, op0=ALU.mult),
                 r=[Ph[g], vecs], w=[Ph[g]])
            A3 = Sa[g % 2].ap.rearrange("p (s x) -> p s x", s=4)
            B3 = Sb[g % 2].ap.rearrange("p (s x) -> p s x", s=4)
            ta, tb = Sa[g % 2], Sb[g % 2]
            S.op(eng, lambda: EN.tensor_tensor(out=A3[:, :, 0:271], in0=P3[:, :, 0:271], in1=P3[:, :, 1:272],
                                               op=ALU.add), r=[Ph[g]], w=[ta])
            cur, curt, oth, otht, ext = A3, ta, B3, tb, 271
            step = 2
            while step < w_:
                e2 = ext - step
                S.op(eng, lambda cur=cur, oth=oth, e2=e2, step=step: EN.tensor_tensor(
                    out=oth[:, :, 0:e2], in0=cur[:, :, 0:e2], in1=cur[:, :, step:step + e2], op=ALU.add),
                    r=[curt], w=[otht])
                cur, curt, oth, otht, ext = oth, otht, cur, curt, e2
                step *= 2
            s0 = 8 - w_ // 2
            ic = invc.ap[:, g * 1024:(g + 1) * 1024].rearrange("p (s x) -> p s x", s=4)
            S.op(eng, lambda cur=cur, oth=oth, s0=s0, ic=ic: EN.tensor_tensor(
                out=oth[:, :, 0:256], in0=cur[:, :, s0:s0 + 256], in1=ic, op=ALU.mult), r=[curt, invc], w=[otht])
            pl = pooled[g].ap.rearrange("p (s x) -> p s x", s=4)
            S.op(eng, lambda oth=oth, pl=pl, P3=P3: EN.tensor_tensor(out=pl, in0=oth[:, :, 0:256],
                                                                      in1=P3[:, :, 8:264], op=ALU.subtract),
                 r=[otht, Ph[g]], w=[pooled[g]])
            psb = 4
            def mmp(g=g, psb=psb):
                last = None
                for th in range(2):
                    last = nc.tensor.matmul(PB[psb + th].ap, wpool.ap[:, g * 128:(g + 1) * 128],
                                            pooled[g].ap[:, th * 512:(th + 1) * 512], start=True, stop=True)
                return last
            S.op("pe", mmp, r=[wpool, pooled[g]], w=[PB[psb], PB[psb + 1]])
            S.op("act", lambda g=g, psb=psb: nc.scalar.activation(out=yT[4 + g].ap, in_=pspan(psb), func=AF.Identity,
                                                                  scale=VC(V_PSC + g)),
                 r=[PB[psb], PB[psb + 1], vecs], w=[yT[4 + g]])
            run_hook()
        for n in range(8):
            psb = (0, 2, 4)[n % 3]
            S.op("pe", proj_fm(wO, n * 128, yT, psb), r=[w_out] + yT, w=[PB[psb], PB[psb + 1]])
            residual_evac(l, 16, n, psb)
            run_hook()
        while hooks:
            run_hook()
        if hook_finish is not None:
            hook_finish()
        return END

    def ffn_prefetch(l, slab_base):
        slabs = [T("slab%d_%d" % (l, i), slab_base + i * 16384, 8192, BF16) for i in range(2)]
        for s_ in range(2):
            b = slabs[s_ % 2]
            S.dma("pool", b.ap.rearrange("p (c n) -> p c n", c=8)[:, :, 0:1024],
                  d_wup[l, s_, :, 0:1024].rearrange("(c p) n -> p c n", p=128), w=[b])
        return slabs

    def ffn(l, base, slab_base, slabs, pre_down=None):
        o = base
        aT = [T("aT%d" % j, o + j * 2048, NT, BF16) for j in range(NJ)]; o += NJ * 2048
        wdn = [T("wdn%d" % i, o + i * 4096, 2048, BF16) for i in range(NJ // 2)]; o += NJ * 2048
        o = slab_base + 32768
        tmp = []
        for i in range(2):
            tmp.append((T("accg%d" % i, o, NT, F32), T("accv%d" % i, o + 4096, NT, F32),
                        T("sg%d" % i, o + 8192, NT, F32)))
            o += 12288
        END = o
        def wdv(j):
            return wdn[j // 2].ap.rearrange("p (j n) -> p j n", j=2)[:, j % 2, :]

        def load_slab(s):
            b = slabs[s % 2]
            ncol = 1024 if s < 5 else 512
            S.dma("pool", b.ap.rearrange("p (c n) -> p c n", c=8)[:, :, 0:ncol],
                  d_wup[l, s, :, 0:ncol].rearrange("(c p) n -> p c n", p=128), w=[b])
        for j in range(NJ):
            s, jj = j // 4, j % 4
            half = 512 if s < 5 else 256
            sv = slabs[s % 2].ap.rearrange("p (c n) -> p c n", c=8)
            if 2 <= j < 2 + NJ // 2:
                q4 = (j - 2) * 2
                S.dma("pool", wdn[q4 // 2].ap.rearrange("p (j n) -> p j n", j=2),
                      d_wdown[l, q4 * 128:(q4 + 2) * 128, :].rearrange("(j p) n -> p j n", p=128),
                      w=[wdn[q4 // 2]])
            pg = (j % 2) * 4
            if j == 0:
                proj_fine(sv, slabs[0], jj * 128, hT, pg)
            else:
                S.op("pe", proj_fm(sv, jj * 128, hT, pg), r=[slabs[s % 2]] + hT, w=[PB[pg], PB[pg + 1]])
            S.op("pe", proj_fm(sv, half + jj * 128, hT, pg + 2), r=[slabs[s % 2]] + hT, w=[PB[pg + 2], PB[pg + 3]])
            if jj == 3 and s + 2 < 6:
                load_slab(s + 2)
            accg, accv, sg = tmp[j % 2]
            for which, acc, pb0 in ((0, accg, pg), (1, accv, pg + 2)):
                ch = which * NJ + j
                z = pspan(pb0)
                cw0 = VC(V_FCW + (l * 3 + 0) * 44 + ch)
                cw1 = VC(V_FCW + (l * 3 + 1) * 44 + ch)
                cw2 = VC(V_FCW + (l * 3 + 2) * 44 + ch)
                cb = VC(V_FCB + l * 44 + ch)
                f0 = fixv.ap[:, (l * 2 + 0) * 44 + ch:(l * 2 + 0) * 44 + ch + 1]
                f2 = fixv.ap[:, (l * 2 + 1) * 44 + ch:(l * 2 + 1) * 44 + ch + 1]
                rp = [PB[pb0], PB[pb0 + 1]]
                S.op("act", lambda acc=acc, z=z, cw1=cw1, cb=cb: nc.scalar.activation(
                    out=acc.ap, in_=z, func=AF.Identity, bias=cb, scale=cw1), r=rp + [vecs], w=[acc])
                S.op("dve", lambda acc=acc, z=z, cw0=cw0: nc.vector.scalar_tensor_tensor(
                    out=acc.ap[:, 1:NT], in0=z[:, 0:NT - 1], scalar=cw0, in1=acc.ap[:, 1:NT],
                    op0=ALU.mult, op1=ALU.add), r=rp + [vecs, acc], w=[acc])
                S.op("dve", lambda acc=acc, z=z, cw2=cw2: nc.vector.scalar_tensor_tensor(
                    out=acc.ap[:, 0:NT - 1], in0=z[:, 1:NT], scalar=cw2, in1=acc.ap[:, 0:NT - 1],
                    op0=ALU.mult, op1=ALU.add), r=rp + [vecs, acc], w=[acc])
                S.op("dve", lambda acc=acc, z=z, f0=f0: nc.vector.scalar_tensor_tensor(
                    out=acc.ap[:, 256:NT:256], in0=z[:, 255:NT - 1:256], scalar=f0, in1=acc.ap[:, 256:NT:256],
                    op0=ALU.mult, op1=ALU.add), r=rp + [fixv, acc], w=[acc])
                S.op("dve", lambda acc=acc, z=z, f2=f2: nc.vector.scalar_tensor_tensor(
                    out=acc.ap[:, 255:NT - 1:256], in0=z[:, 256:NT:256], scalar=f2, in1=acc.ap[:, 255:NT - 1:256],
                    op0=ALU.mult, op1=ALU.add), r=rp + [fixv, acc], w=[acc])
            S.op("act", lambda accg=accg, sg=sg: nc.scalar.activation(out=sg.ap, in_=accg.ap, func=AF.Silu),
                 r=[accg], w=[sg])
            S.op("pool", lambda sg=sg, accv=accv, j=j: nc.gpsimd.tensor_tensor(out=aT[j].ap, in0=sg.ap, in1=accv.ap,
                                                                                op=ALU.mult),
                 r=[sg, accv], w=[aT[j]])
        if pre_down is not None:
            pre_down()
        for n in range(8):
            psb = (n % 2) * 2
            def mmd(n=n, psb=psb):
                last = None
                for th in range(2):
                    for j in range(NJ):
                        last = nc.tensor.matmul(PB[psb + th].ap, wdv(j)[:, n * 128:(n + 1) * 128],
                                                aT[j].ap[:, th * 512:(th + 1) * 512],
                                                start=(j == 0), stop=(j == NJ - 1))
                return last
            S.op("pe", mmd, r=wdn + aT, w=[PB[psb], PB[psb + 1]])
            residual_evac(l, 40, n, psb)
        return END

    def odd_mixer(l, base):
        M = base
        w_inA = T("ow_inA", M + 0, 8192, BF16)
        wgate = T("wgate", M + 16384, 2048, BF16)
        o = M + 20480
        xcs = [T("xc%d" % i, o + i * 4096, NT, F32) for i in range(2)]; o += 8192
        xcb = T("xcb", o, NT, BF16); o += 2048
        ggr = T("ggr", o, NT, F32); o += 4096
        rgt = []
        for d_ in range(2):
            tl_ = []
            for k in range(5):
                if d_ == 1 and k >= 3:
                    tl_.append(T("rg%d_%d" % (d_, k), M + 142336 + (k - 3) * 4096, NT, F32))
                else:
                    tl_.append(T("rg%d_%d" % (d_, k), o, NT, F32)); o += 4096
            rgt.append(tl_)
        assert o <= M + 67584
        ycT = [T("ycT%d" % c, M + 67584 + c * 2048, NT, BF16) for c in range(4)]
        w_inB = T("ow_inB", M + 75776, 5120, BF16)
        wkr = T("wkr", M + 86016, 768, BF16)
        wkrs = T("wkrs", M + 87552, 768, BF16)
        wuq = T("wuq", M + 89088, 2304, BF16)
        wuqs = T("wuqs", M + 93696, 2304, BF16)
        wuk = T("wuk", M + 98304, 1024, BF16)
        wuv = T("wuv", M + 100352, 1024, BF16)
        w_out = T("ow_out", M + 102400, 8192, BF16)
        ropeC = T("ropeC", M + 126976, NT, F32, 64, 96)
        ropeS = T("ropeS", M + 131072, NT, F32, 64, 96)
        KRall = T("KRall", M + 135168, 1536, BF16, 64, 96)
        krout = T("krout", M + 138240, NT, F32, 64, 96)
        END = M + 150528

        def cast_rows(dst_tile, src, kc, ncol):
            if dst_tile.name in skip:
                return
            S.dma("pool", dst_tile.ap.rearrange("p (c n) -> p c n", c=kc),
                  src.rearrange("(c p) n -> p c n", p=128), w=[dst_tile])
        cast_rows(w_inA, d_owin_a, 8, 1024)
        for i in range(2):
            S.dma("pool", wgate.ap[:, i * 1024:(i + 1) * 1024], d_wgate[:, i * 1024:(i + 1) * 1024], w=[wgate])
        cast_rows(w_inB, d_owin_b, 8, 640)
        cast_rows(wkr, d_wkr, 8, 96)
        cast_rows(wkrs, d_wkrs, 8, 96)
        cast_rows(wuq, d_wuq, 3, 768)
        cast_rows(wuqs, d_wuqs, 3, 768)
        cast_rows(wuk, d_wuk, 2, 512)
        cast_rows(wuv, d_wuv, 2, 512)
        cast_rows(w_out, d_owout, 8, 1024)
        S.dma("sp", ropeC.ap, d_ropeC, w=[ropeC])
        S.dma("sp", ropeS.ap, d_ropeS, w=[ropeS])
        S.dma("sp", krout.ap[:, 0:512], d_krc, w=[krout])
        S.op("pool", lambda: nc.gpsimd.tensor_copy(out=KRall.ap[:, 0:512], in_=krout.ap[:, 0:512]),
             r=[krout], w=[KRall])

        S.op("act", lambda: nc.scalar.activation(out=clam.ap[:, 0:8], in_=VC(V_LAM, 8), func=AF.Exp, scale=-1.0),
             r=[vecs], w=[clam])
        S.op("act", lambda: nc.scalar.activation(out=clam.ap[:, 0:8], in_=clam.ap[:, 0:8], func=AF.Ln, bias=1.0),
             r=[clam], w=[clam])
        S.op("dve", lambda: nc.vector.tensor_scalar(out=clam.ap[:, 8:16], in0=clam.ap[:, 0:8], scalar1=-16.0,
                                                     scalar2=None, op0=ALU.mult), r=[clam], w=[clam])
        S.op("dve", lambda: nc.vector.tensor_scalar(out=clam.ap[:, 0:8], in0=clam.ap[:, 0:8], scalar1=-8.0,
                                                     scalar2=None, op0=ALU.mult), r=[clam], w=[clam])

        if debug == "canary":
            out_events.append(S.dma("sp", o_dbg[:, 4400:4400 + NV], vecs.ap, r=[vecs]))
            S.wait_all("sp", out_events)
            return END
        if debug == "hT":
            for c in range(4):
                S.op("dve", lambda: nc.vector.tensor_copy(out=xc.ap, in_=hT[c].ap), r=[hT[c]], w=[xc])
                out_events.append(S.dma("sp", o_dbg[:, c * 1024:(c + 1) * 1024], xc.ap, r=[xc]))
            out_events.append(S.dma("sp", o_dbg[:, 4096:4096 + 48], modT[1].ap, r=[modT[1]]))
            out_events.append(S.dma("sp", o_dbg[:, 4200:4264], small[0].ap, r=[small[0]]))
            out_events.append(S.dma("sp", o_dbg[:, 4300:4364], small[1].ap, r=[small[1]]))
            out_events.append(S.dma("sp", o_dbg[:, 4400:4400 + NV], vecs.ap, r=[vecs]))
            out_events.append(S.dma("sp", o_dbg[:, 5000:5016], clam.ap, r=[clam]))
            for c in range(4):
                out_events.append(S.dma("sp", o_dbg[:, 6000 + c * 1024:6000 + (c + 1) * 1024], xT[c].ap, r=[xT[c]]))
        wA = w_inA.ap.rearrange("p (c n) -> p c n", c=8)
        wB = w_inB.ap.rearrange("p (c n) -> p c n", c=8)
        for c in range(4):
            xc = xcs[c % 2]
            if c == 0:
                proj_fine(wA, w_inA, 0, hT, 0)
            else:
                S.op("pe", proj_fm(wA, c * 128, hT, 0), r=[w_inA] + hT, w=[PB[0], PB[1]])
            z = pspan(0)
            rp = [PB[0], PB[1]]
            cw = [VC(V_RGW + tap * 4 + c) for tap in range(4)]
            fx = [fixv.ap[:, 176 + tap * 4 + c:177 + tap * 4 + c] for tap in range(4)]
            S.op("act", lambda: nc.scalar.activation(out=xc.ap, in_=z, func=AF.Identity, bias=VC(V_RGB + c),
                                                     scale=cw[1]), r=rp + [vecs], w=[xc])
            S.op("dve", lambda: nc.vector.scalar_tensor_tensor(out=xc.ap[:, 1:NT], in0=z[:, 0:NT - 1], scalar=cw[0],
                                                                in1=xc.ap[:, 1:NT], op0=ALU.mult, op1=ALU.add),
                 r=rp + [vecs, xc], w=[xc])
            S.op("dve", lambda: nc.vector.scalar_tensor_tensor(out=xc.ap[:, 0:NT - 1], in0=z[:, 1:NT], scalar=cw[2],
                                                                in1=xc.ap[:, 0:NT - 1], op0=ALU.mult, op1=ALU.add),
                 r=rp + [vecs, xc], w=[xc])
            S.op("dve", lambda: nc.vector.scalar_tensor_tensor(out=xc.ap[:, 0:NT - 2], in0=z[:, 2:NT], scalar=cw[3],
                                                                in1=xc.ap[:, 0:NT - 2], op0=ALU.mult, op1=ALU.add),
                 r=rp + [vecs, xc], w=[xc])
            S.op("dve", lambda: nc.vector.scalar_tensor_tensor(out=xc.ap[:, 256:NT:256], in0=z[:, 255:NT - 1:256],
                                                                scalar=fx[0], in1=xc.ap[:, 256:NT:256],
                                                                op0=ALU.mult, op1=ALU.add), r=rp + [fixv, xc], w=[xc])
            S.op("dve", lambda: nc.vector.scalar_tensor_tensor(out=xc.ap[:, 255:NT - 1:256], in0=z[:, 256:NT:256],
                                                                scalar=fx[2], in1=xc.ap[:, 255:NT - 1:256],
                                                                op0=ALU.mult, op1=ALU.add), r=rp + [fixv, xc], w=[xc])
            x3 = xc.ap.rearrange("p (s x) -> p s x", s=4)
            z3 = z.rearrange("p (s x) -> p s x", s=4)
            S.op("dve", lambda: nc.vector.scalar_tensor_tensor(out=x3[:, 0:3, 254:256], in0=z3[:, 1:4, 0:2],
                                                                scalar=fx[3], in1=x3[:, 0:3, 254:256],
                                                                op0=ALU.mult, op1=ALU.add), r=rp + [fixv, xc], w=[xc])
            S.op("dve", lambda: nc.vector.tensor_copy(out=xcb.ap, in_=xc.ap), r=[xc], w=[xcb])
            S.op("pe", proj_fm(wA, 512 + c * 128, hT, 2), r=[w_inA] + hT, w=[PB[2], PB[3]])
            S.op("act", lambda: nc.scalar.activation(out=ggr.ap, in_=pspan(2), func=GELU), r=[PB[2], PB[3]], w=[ggr])
            for d_ in range(2):
                r_, i_ = rgt[d_][0], rgt[d_][1]
                def mmg(d_=d_):
                    last = None
                    for kind in range(2):
                        col = (c * 4 + d_ * 2 + kind) * 128
                        for th in range(2):
                            last = nc.tensor.matmul(PB[4 + kind * 2 + th].ap, wgate.ap[:, col:col + 128],
                                                    xcb.ap[:, th * 512:(th + 1) * 512], start=True, stop=True)
                    return last
                S.op("pe", mmg, r=[wgate, xcb], w=[PB[4], PB[5], PB[6], PB[7]])
                S.op("act", lambda: nc.scalar.activation(out=r_.ap, in_=pspan(4), func=AF.Sigmoid,
                                                         bias=VC(V_BA + d_ * 4 + c)), r=[PB[4], PB[5], vecs], w=[r_])
                S.op("act", lambda: nc.scalar.activation(out=i_.ap, in_=pspan(6), func=AF.Sigmoid,
                                                         bias=VC(V_BX + d_ * 4 + c)), r=[PB[6], PB[7], vecs], w=[i_])
            for d_ in range(2):
                r_, a_, m_ = rgt[d_][0], rgt[d_][2], rgt[d_][3]
                S.op("act", lambda: nc.scalar.activation(out=a_.ap, in_=r_.ap, func=AF.Exp,
                                                         scale=clam.ap[:, d_ * 4 + c:d_ * 4 + c + 1]),
                     r=[r_, clam], w=[a_])
                S.op("act", lambda: nc.scalar.activation(out=m_.ap, in_=r_.ap, func=AF.Exp,
                                                         scale=clam.ap[:, 8 + d_ * 4 + c:8 + d_ * 4 + c + 1]),
                     r=[r_, clam], w=[m_])
                S.op("dve", lambda: nc.vector.tensor_scalar(out=m_.ap, in0=m_.ap, scalar1=-1.0, scalar2=0.0,
                                                             op0=ALU.add, op1=ALU.min), r=[m_], w=[m_])
            for d_ in range(2):
                m_ = rgt[d_][3]
                S.op("act", lambda: nc.scalar.activation(out=m_.ap, in_=m_.ap, func=AF.Sqrt, scale=-1.0),
                     r=[m_], w=[m_])
            for d_ in range(2):
                i_, a_, m_, h_ = rgt[d_][1], rgt[d_][2], rgt[d_][3], rgt[d_][4]
                S.op("dve", lambda: nc.vector.tensor_tensor(out=i_.ap, in0=i_.ap, in1=xc.ap, op=ALU.mult),
                     r=[i_, xc], w=[i_])
                S.op("dve", lambda: nc.vector.tensor_tensor(out=m_.ap, in0=m_.ap, in1=i_.ap, op=ALU.mult),
                     r=[m_, i_], w=[m_])
                bcols = a_.ap[:, 256:NT:256] if d_ == 0 else a_.ap[:, 255:NT - 1:256]
                S.op("dve", lambda: nc.vector.tensor_scalar(out=bcols, in0=bcols, scalar1=VC(V_MASK), scalar2=None,
                                                             op0=ALU.mult), r=[a_, vecs], w=[a_])
                if d_ == 0:
                    S.op("dve", lambda: nc.vector.tensor_tensor_scan(out=h_.ap, data0=a_.ap, data1=m_.ap,
                                                                      initial=VC(V_ST + c), op0=ALU.mult,
                                                                      op1=ALU.add), r=[a_, m_, vecs], w=[h_])
                    S.op("pool", lambda: nc.gpsimd.tensor_copy(out=st_t.ap[:, c * 8:c * 8 + 4],
                                                               in_=h_.ap[:, 255:NT:256]), r=[h_], w=[st_t])
                else:
                    S.op("dve", lambda: nc.vector.tensor_tensor_scan(out=h_.ap[:, ::-1], data0=a_.ap[:, ::-1],
                                                                      data1=m_.ap[:, ::-1],
                                                                      initial=VC(V_ST + 4 + c), op0=ALU.mult,
                                                                      op1=ALU.add), r=[a_, m_, vecs], w=[h_])
                    S.op("pool", lambda: nc.gpsimd.tensor_copy(out=st_t.ap[:, c * 8 + 4:c * 8 + 8],
                                                               in_=h_.ap[:, 0:NT:256]), r=[h_], w=[st_t])
            hf, hb = rgt[0][4], rgt[1][4]
            S.op("dve", lambda: nc.vector.tensor_tensor(out=hf.ap, in0=hf.ap, in1=hb.ap, op=ALU.add),
                 r=[hf, hb], w=[hf])
            S.op("dve", lambda: nc.vector.tensor_tensor(out=ycT[c].ap, in0=hf.ap, in1=ggr.ap, op=ALU.mult),
                 r=[hf, ggr], w=[ycT[c]])
        out_events.append(S.dma("sp", o_st, st_t.ap, r=[st_t]))
        for c in range(4):
            dump("yc%d" % c, ycT[c])

        o = M
        qlat = [T("qlat%d" % i, o + i * 4096, NT, F32) for i in range(3)]; o += 12288
        ckvf = [T("ckvf%d" % i, o + i * 4096, NT, F32) for i in range(2)]; o += 8192
        qn = [T("qn%d" % i, o + i * 2048, NT, BF16) for i in range(3)]; o += 6144
        ckva = [T("ckva%d" % i, o + i * 3072, 1536, BF16) for i in range(2)]; o += 6144
        Kb = [T("Kb%d" % i, o + i * 3072, 1536, BF16) for i in range(2)]; o += 6144
        Qb = [T("Qb%d" % i, o + i * 2048, NT, BF16) for i in range(2)]; o += 4096
        Pt = [T("Pt%d" % i, o + i * 1024, 512, BF16) for i in range(4)]; o += 4096
        rb = T("rb", o, 512, F32); o += 2048
        rtA = T("rtA", o, NT, F32, 64, 96); o += 4096
        rtB = T("rtB", o, NT, F32, 64, 96); o += 4096
        VT_EXTRA = o
        o += 4 * 1536 + 2048
        assert o <= M + 67584

        for rc in range(2):
            S.dma("pool", ckva[rc].ap[:, 0:512], d_ckvc[rc * 128:(rc + 1) * 128, :], w=[ckva[rc]])
        for i in range(2):
            S.op("pool", lambda: nc.gpsimd.memset(Kb[i].ap[96:128, :], 0.0), w=[Kb[i]])
            S.op("pool", lambda: nc.gpsimd.memset(Qb[i].ap[96:128, :], 0.0), w=[Qb[i]])
        mstg = T("mstg", rtA.b0 - M + M, 1536, F32, 96, 100)
        S.dma("sp", mstg.ap, d_maskk, w=[mstg])
        for i in range(2):
            S.op("pool", lambda: nc.gpsimd.tensor_copy(out=Kb[i].ap[96:100, :], in_=mstg.ap), r=[mstg], w=[Kb[i]])
        S.dma("sp", mstg.ap[:, 0:NT], d_maskq, w=[mstg])
        for i in range(2):
            S.op("pool", lambda: nc.gpsimd.tensor_copy(out=Qb[i].ap[96:100, :], in_=mstg.ap[:, 0:NT]),
                 r=[mstg], w=[Qb[i]])

        ssq = T("ssq", M + 142336, NT, F32)
        sqb = [T("sqb%d" % i, M + 146432 + i * 2048, NT, BF16) for i in range(2)]

        def rms_block(nchunks, col0, gcol, dst_f, ps_acc, ndim):
            for qc in range(nchunks):
                psb = (qc % 2) * 2
                S.op("pe", proj_fm(wB, col0 + qc * 128, hT, psb), r=[w_inB] + hT, w=[PB[psb], PB[psb + 1]])
                sq = sqb[qc % 2]
                S.op("act", lambda: nc.scalar.activation(out=dst_f[qc].ap, in_=pspan(psb), func=AF.Identity),
                     r=[PB[psb], PB[psb + 1]], w=[dst_f[qc]])
                S.op("act", lambda: nc.scalar.activation(out=sq.ap, in_=pspan(psb), func=AF.Square),
                     r=[PB[psb], PB[psb + 1]], w=[sq])
                def mm(qc=qc, sq=sq):
                    last = None
                    for th in range(2):
                        last = nc.tensor.matmul(PB[ps_acc + th].ap, ones_bf.ap, sq.ap[:, th * 512:(th + 1) * 512],
                                                start=(qc == 0), stop=(qc == nchunks - 1))
                    return last
                S.op("pe", mm, r=[sq, ones_bf], w=[PB[ps_acc], PB[ps_acc + 1]])
            S.op("act", lambda: nc.scalar.activation(out=ssq.ap, in_=pspan(ps_acc), func=AF.Sqrt,
                                                     bias=epsrms.ap[:, 0:1], scale=1.0 / ndim),
                 r=[PB[ps_acc], PB[ps_acc + 1], epsrms], w=[ssq])
            S.op("dve", lambda: nc.vector.reciprocal(out=ssq.ap, in_=ssq.ap), r=[ssq], w=[ssq])

        rms_block(3, 0, V_QG, qlat, 4, 384.0)
        for qc in range(3):
            S.op("dve", lambda: nc.vector.scalar_tensor_tensor(out=qn[qc].ap, in0=qlat[qc].ap, scalar=VC(V_QG + qc),
                                                                in1=ssq.ap, op0=ALU.mult, op1=ALU.mult),
                 r=[qlat[qc], vecs, ssq], w=[qn[qc]])
        rms_block(2, 384, V_KVG, ckvf, 6, 256.0)
        for rc in range(2):
            S.op("dve", lambda: nc.vector.scalar_tensor_tensor(out=ckvf[rc].ap, in0=ckvf[rc].ap,
                                                                scalar=VC(V_KVG + rc), in1=ssq.ap,
                                                                op0=ALU.mult, op1=ALU.mult),
                 r=[ckvf[rc], vecs, ssq], w=[ckvf[rc]])
            out_events.append(S.dma("sp", o_ckvT[rc * 128:(rc + 1) * 128, :], ckvf[rc].ap, r=[ckvf[rc]]))
            S.op("pool", lambda: nc.gpsimd.tensor_copy(out=ckva[rc].ap[:, 512:1536], in_=ckvf[rc].ap),
                 r=[ckvf[rc]], w=[ckva[rc]])
        wk3 = wkr.ap.rearrange("p (c n) -> p c n", c=8)
        wks3 = wkrs.ap.rearrange("p (c n) -> p c n", c=8)
        for wv_, psb, tl in ((wk3, 0, wkr), (wks3, 2, wkrs)):
            def mm(wv_=wv_, psb=psb):
                last = None
                for th in range(2):
                    for kc in range(8):
                        last = nc.tensor.matmul(PB[psb + th].ap[0:96, :], wv_[:, kc, :],
                                                hT[kc].ap[:, th * 512:(th + 1) * 512], start=(kc == 0), stop=(kc == 7))
                return last
            S.op("pe", mm, r=[tl] + hT, w=[PB[psb], PB[psb + 1]])
        S.op("act", lambda: nc.scalar.activation(out=krout.ap, in_=pspan(0)[64:96, :], func=AF.Identity),
             r=[PB[0], PB[1]], w=[krout])
        out_events.append(S.dma("sp", o_krT, krout.ap, r=[krout]))
        S.op("dve", lambda: nc.vector.tensor_tensor(out=rtA.ap, in0=pspan(0)[64:96, :], in1=ropeC.ap, op=ALU.mult),
             r=[PB[0], PB[1], ropeC], w=[rtA])
        S.op("dve", lambda: nc.vector.tensor_tensor(out=rtB.ap, in0=pspan(2)[64:96, :], in1=ropeS.ap, op=ALU.mult),
             r=[PB[2], PB[3], ropeS], w=[rtB])
        S.op("pool", lambda: nc.gpsimd.tensor_tensor(out=KRall.ap[:, 512:1536], in0=rtA.ap, in1=rtB.ap, op=ALU.add),
             r=[rtA, rtB], w=[KRall])
        for i in range(2):
            S.op("pool", lambda: nc.gpsimd.tensor_copy(out=Kb[i].ap[64:96, :], in_=KRall.ap), r=[KRall], w=[Kb[i]])

        wv3 = wuv.ap.rearrange("p (c n) -> p c n", c=2)
        Vt = [T("Vt%d" % i, (M + 75776 + i * 1536) if i < 8 else (VT_EXTRA + (i - 8) * 1536), 768, BF16)
              for i in range(12)]
        for kc in range(12):
            psb = kc % 2
            def mm(kc=kc, psb=psb):
                last = None
                for rc in range(2):
                    last = nc.tensor.matmul(PB[psb].ap, ckva[rc].ap[:, kc * 128:(kc + 1) * 128], wv3[:, rc, :],
                                            start=(rc == 0), stop=(rc == 1))
                return last
            S.op("pe", mm, r=[wuv] + ckva, w=[PB[psb]])
            V3 = Vt[kc].ap.rearrange("p (c x) -> p c x", c=4)
            P4 = PB[psb].ap.rearrange("p (c t e) -> p c t e", c=4, t=2)
            S.op("pool", lambda: nc.gpsimd.memset(V3[:, :, 64:128], 0.0), w=[Vt[kc]])
            S.op("act", lambda: nc.scalar.activation(out=V3[:, :, 0:64], in_=P4[:, :, 0, :], func=AF.Identity),
                 r=[PB[psb]], w=[Vt[kc]])
            S.op("act", lambda: nc.scalar.activation(out=V3[:, :, 128:192], in_=P4[:, :, 1, :], func=AF.Identity),
                 r=[PB[psb]], w=[Vt[kc]])
        ydT = [T("ydT%d" % c, M + c * 2048, NT, BF16) for c in range(4)]
        wq3 = wuq.ap.rearrange("p (c n) -> p c n", c=3)
        wqs3 = wuqs.ap.rearrange("p (c n) -> p c n", c=3)
        wk3_ = wuk.ap.rearrange("p (c n) -> p c n", c=2)
        def head_proj(h):
            kb, qb = Kb[h % 2], Qb[h % 2]
            groups = []

            def kgroup(kblk):
                def mm():
                    last = None
                    for rc in range(2):
                        last = nc.tensor.matmul(PB[7].ap[0:64, :], wk3_[:, rc, h * 64:(h + 1) * 64],
                                                ckva[rc].ap[:, kblk * 512:(kblk + 1) * 512],
                                                start=(rc == 0), stop=(rc == 1))
                    return last
                S.op("pe", mm, r=[wuk] + ckva, w=[PB[7]])
                S.op("dve", lambda: nc.vector.tensor_copy(out=kb.ap[0:64, kblk * 512:(kblk + 1) * 512],
                                                          in_=PB[7].ap[0:64, :]),
                     r=[PB[7]], w=[kb])

            def qgroup(th, sw):
                ts_ = slice(th * 512, (th + 1) * 512)
                wv_, tl = (wq3, wuq) if sw == 0 else (wqs3, wuqs)

                def mm():
                    last = None
                    for qc in range(3):
                        last = nc.tensor.matmul(PB[7].ap[0:96, :], wv_[:, qc, h * 96:(h + 1) * 96],
                                                qn[qc].ap[:, th * 512:(th + 1) * 512],
                                                start=(qc == 0), stop=(qc == 2))
                    return last
                S.op("pe", mm, r=[tl] + qn, w=[PB[7]])
                if sw == 0:
                    S.op("dve", lambda: nc.vector.tensor_copy(out=qb.ap[0:64, ts_], in_=PB[7].ap[0:64, :]),
                         r=[PB[7]], w=[qb])
                    S.op("dve", lambda: nc.vector.tensor_tensor(out=rtA.ap[:, ts_], in0=PB[7].ap[64:96, :],
                                                                 in1=ropeC.ap[:, ts_], op=ALU.mult),
                         r=[PB[7], ropeC], w=[rtA])
                else:
                    S.op("dve", lambda: nc.vector.tensor_tensor(out=rtB.ap[:, ts_], in0=PB[7].ap[64:96, :],
                                                                 in1=ropeS.ap[:, ts_], op=ALU.mult),
                         r=[PB[7], ropeS], w=[rtB])
                    S.op("pool", lambda: nc.gpsimd.tensor_tensor(out=qb.ap[64:96, ts_], in0=rtA.ap[:, ts_],
                                                                  in1=rtB.ap[:, ts_], op=ALU.add),
                         r=[rtA, rtB], w=[qb])
            for kblk in range(3):
                groups.append(lambda kblk=kblk: kgroup(kblk))
            for th in range(2):
                for sw in range(2):
                    groups.append(lambda th=th, sw=sw: qgroup(th, sw))
            return groups

        steps = [(h, th, kc) for h in range(8) for th in range(2) for kc in range(12)]
        LA = 2
        SBANKS = (0, 1, 2)

        def emit_S(i):
            h, th, kc = steps[i]
            kb, qb = Kb[h % 2], Qb[h % 2]
            sb = SBANKS[i % 3]
            S.op("pe", lambda: nc.tensor.matmul(PB[sb].ap, kb.ap[:, kc * 128:(kc + 1) * 128],
                                                qb.ap[:, th * 512:(th + 1) * 512], start=True, stop=True),
                 r=[kb, qb], w=[PB[sb]])

        def emit_rest(i):
            h, th, kc = steps[i]
            blk = h * 2 + th
            ts_ = slice(th * 512, (th + 1) * 512)
            sb = SBANKS[i % 3]
            pt = Pt[i % 4]
            ob = 3 + (blk % 2)
            db = 5 + (blk % 2)
            S.op("act", lambda: nc.scalar.activation(out=pt.ap, in_=PB[sb].ap, func=AF.Exp, scale=ATTN_SCALE),
                 r=[PB[sb]], w=[pt])
            hb_ = (h // 2) * 192
            if h % 2 == 0:
                vl = Vt[kc].ap[:, hb_:hb_ + 64]
                ol = ones_bf.ap[:, 0:64]
                prt = slice(0, 64)
            else:
                vl = Vt[kc].ap[:, hb_ + 64:hb_ + 192]
                ol = zo.ap
                prt = slice(0, 128)
            def mmpv():
                nc.tensor.matmul(PB[ob].ap[prt, :], vl, pt.ap, start=(kc == 0), stop=(kc == 11))
                return nc.tensor.matmul(PB[db].ap[prt, :], ol, pt.ap, start=(kc == 0), stop=(kc == 11))
            S.op("pe", mmpv, r=[Vt[kc], pt, ones_bf, zo], w=[PB[ob], PB[db]])
            if kc == 11:
                pr2 = slice(0, 64) if h % 2 == 0 else slice(64, 128)
                rb_ = rbs[blk % 2]
                S.op("dve", lambda: nc.vector.reciprocal(out=rb_.ap[pr2, :], in_=PB[db].ap[pr2, :]),
                     r=[PB[db]], w=[rb_])
                S.op("dve", lambda: nc.vector.tensor_tensor(out=ydT[h // 2].ap[pr2, ts_], in0=PB[ob].ap[pr2, :],
                                                             in1=rb_.ap[pr2, :], op=ALU.mult),
                     r=[PB[ob], rb_], w=[ydT[h // 2]])

        rbs = [rb, T("rb2", VT_EXTRA + 4 * 1536, 512, F32)]
        for g_ in head_proj(0):
            g_()
        pending = []
        for i in range(len(steps) + LA):
            if i < len(steps):
                h, th, kc = steps[i]
                if th == 0 and kc == 0:
                    assert not pending
                    pending = head_proj(h + 1) if h + 1 < 8 else []
                emit_S(i)
                if pending and (th * 12 + kc) % 3 == 1:
                    pending.pop(0)()
            if i - LA >= 0:
                emit_rest(i - LA)
        for c in range(4):
            dump("yd_c%d" % c, ydT[c])
        wO = w_out.ap.rearrange("p (c n) -> p c n", c=8)
        ysrc = ycT + ydT
        for n in range(8):
            psb = (n % 2) * 2
            S.op("pe", proj_fm(wO, n * 128, ysrc, psb), r=[w_out] + ysrc, w=[PB[psb], PB[psb + 1]])
            residual_evac(l, 16, n, psb)
        return END

    SLAB_BASE = ARENA - 32768 - 24576
    even_mixer(0, MIX_BASE, hook_setup=lambda off: mod_setup(1, off, 512),
               pre_hooks=mod0_pieces[4:], pre_finish=mod0_finish.b)
    hb_next(0)
    slabs0 = ffn_prefetch(0, SLAB_BASE)
    layernorm(0, 1, MIX_BASE)
    if stop_after == "mix0":
        for c in range(8):
            out_events.append(S.dma("sp", o_yT[c * 128:(c + 1) * 128, :], xT[c].ap, r=[xT[c]]))
    else:
        ffn(0, MIX_BASE, SLAB_BASE, slabs0)
        layernorm(0, 2, MIX_BASE)
        if stop_after == "ffn0":
            for c in range(8):
                out_events.append(S.dma("sp", o_yT[c * 128:(c + 1) * 128, :], xT[c].ap, r=[xT[c]]))
        else:
            odd_mixer(1, MIX_BASE)
            slabs1 = ffn_prefetch(1, SLAB_BASE)
            layernorm(1, 1, MIX_BASE)
            if stop_after == "mix1":
                for c in range(8):
                    out_events.append(S.dma("sp", o_yT[c * 128:(c + 1) * 128, :], xT[c].ap, r=[xT[c]]))
            else:
                ffn(1, MIX_BASE, SLAB_BASE, slabs1)
                layernorm(1, 2, MIX_BASE, final=True)
    if debug == "canary_end":
        out_events.append(S.dma("sp", o_dbg[:, 4400:4400 + NV], vecs.ap, r=[vecs]))
        out_events.append(S.dma("sp", o_dbg[:, 5000:5016], clam.ap, r=[clam]))
    S.wait_all("sp", out_events)
    build_program.nwaits = S.nwaits
    return nc


def _pcols(vec):
    vec = np.asarray(vec, np.float32).reshape(-1, 128)
    return np.ascontiguousarray(vec.T)


def _consts(kind):
    t = np.arange(NT)
    if kind == "prompt":
        seg = t // 256
        tau = t % 256
        L = 256
    else:
        seg = np.zeros(NT, np.int64)
        tau = t
        L = NT
    invcnt = np.zeros((4, NT), np.float32)
    for gi, w in enumerate((2, 4, 8, 16)):
        lo = np.clip(tau - w // 2, 0, L)
        hi = np.clip(tau + w - w // 2, 0, L)
        invcnt[gi] = 1.0 / (hi - lo).astype(np.float32)
    maskq = np.zeros((4, NT), np.float32)
    maskk = np.zeros((4, 1536), np.float32)
    if kind == "prompt":
        for s in range(4):
            maskq[s, seg == s] = 1.0
            maskk[s, :] = NEG
            maskk[s, 512 + np.nonzero(seg == s)[0]] = 0.0
    else:
        maskq[0, :] = 1.0
    ropeC = np.ones((32, NT), np.float32)
    ropeS = np.zeros((32, NT), np.float32)
    if kind == "sample":
        half = 16
        inv = (10000.0 ** (-np.arange(0, half, 2, dtype=np.float32) / half)).astype(np.float32)
        r = (t // 64).astype(np.float32)
        col = (t % 64).astype(np.float32)
        ang = np.concatenate([r[:, None] * inv, col[:, None] * inv], axis=-1).astype(np.float32)
        cos = np.cos(ang).astype(np.float32).T
        sin = np.sin(ang).astype(np.float32).T
        ropeC[0::2] = cos
        ropeC[1::2] = cos
        ropeS[0::2] = -sin
        ropeS[1::2] = sin
    return dict(invcnt=np.ascontiguousarray(np.broadcast_to(invcnt.reshape(1, 4096), (128, 4096))),
                maskq=maskq, maskk=maskk, ropeC=ropeC, ropeS=ropeS)


def prep_inputs(inp):
    f = lambda k: np.asarray(inp[k], np.float32)
    shared = {}
    shared["w_mod"] = f("w_mod")
    shared["even_w_in"] = f("even_w_in")[0]
    shared["even_w_out"] = f("even_w_out")[0]
    shared["w_sT"] = np.ascontiguousarray(f("even_w_s")[0].transpose(2, 0, 1).reshape(128, 512))
    shared["b_s"] = f("even_b_s")[0].reshape(1, 512)
    shared["w_pool"] = np.ascontiguousarray(f("even_w_pool")[0].transpose(1, 0, 2).reshape(128, 512))
    owin = f("odd_w_in")[0]
    shared["odd_w_in_a"] = np.ascontiguousarray(owin[:, 0:1024])
    shared["odd_w_in_b"] = np.ascontiguousarray(owin[:, 1024:1664])
    wkr = owin[:, 1664:1696]
    pad = np.zeros((D, 96), np.float32)
    pad[:, 64:96] = wkr
    shared["w_kr_pad"] = pad
    swap = np.arange(32) ^ 1
    pads = np.zeros((D, 96), np.float32)
    pads[:, 64:96] = wkr[:, swap]
    shared["w_kr_swap"] = pads
    wg = np.zeros((128, 4, 4, 128), np.float32)
    wa, wx = f("rg_w_a")[0], f("rg_w_x")[0]
    for c in range(4):
        for k, wsrc in enumerate((wa[0], wx[0], wa[1], wx[1])):
            wg[0:64, c, k, 0:64] = wsrc[2 * c]
            wg[64:128, c, k, 64:128] = wsrc[2 * c + 1]
    shared["w_gate"] = wg.reshape(128, 2048)
    wuq = f("mla_w_uq")[0]
    shared["w_uq"] = np.ascontiguousarray(wuq.reshape(384, 768))
    wuqs = np.zeros((384, 8, 96), np.float32)
    wuqs[:, :, 64:96] = wuq[:, :, 64 + swap]
    shared["w_uq_swap"] = wuqs.reshape(384, 768)
    shared["w_uk"] = np.ascontiguousarray(f("mla_w_uk")[0].reshape(256, 512))
    shared["w_uv"] = np.ascontiguousarray(f("mla_w_uv")[0].reshape(256, 512))
    shared["odd_w_out"] = f("odd_w_out")[0]
    wup = f("ffn_w_up")
    wupr = np.zeros((2, 6, D, 1024), np.float32)
    for s in range(5):
        wupr[:, s, :, 0:512] = wup[:, :, s * 512:(s + 1) * 512]
        wupr[:, s, :, 512:1024] = wup[:, :, D_FF + s * 512:D_FF + (s + 1) * 512]
    wupr[:, 5, :, 0:256] = wup[:, :, 2560:2816]
    wupr[:, 5, :, 256:512] = wup[:, :, D_FF + 2560:D_FF + 2816]
    shared["ffn_w_up_r"] = wupr
    shared["ffn_w_down"] = f("ffn_w_down")

    vec = np.zeros((128, NV), np.float32)
    bm = f("b_mod")
    for l in range(2):
        vec[:, V_BMOD + l * 48:V_BMOD + (l + 1) * 48] = _pcols(bm[l])
        for i, k in enumerate(("ln1_g", "ln1_b", "ln2_g", "ln2_b")):
            vec[:, V_LN + l * 32 + i * 8:V_LN + l * 32 + (i + 1) * 8] = _pcols(f(k)[l])
        fcw = f("ffn_conv_w")[l]
        for tap in range(3):
            vec[:, V_FCW + (l * 3 + tap) * 44:V_FCW + (l * 3 + tap + 1) * 44] = _pcols(fcw[tap])
        vec[:, V_FCB + l * 44:V_FCB + (l + 1) * 44] = _pcols(f("ffn_conv_b")[l])
    vec[:, V_PSC:V_PSC + 4] = _pcols(f("even_pool_scale")[0])
    rgw = f("rg_conv_w")[0]
    for tap in range(4):
        vec[:, V_RGW + tap * 4:V_RGW + (tap + 1) * 4] = _pcols(rgw[tap])
    vec[:, V_RGB:V_RGB + 4] = _pcols(f("rg_conv_b")[0])
    for d_ in range(2):
        vec[:, V_BA + d_ * 4:V_BA + (d_ + 1) * 4] = _pcols(f("rg_b_a")[0, d_])
        vec[:, V_BX + d_ * 4:V_BX + (d_ + 1) * 4] = _pcols(f("rg_b_x")[0, d_])
        vec[:, V_LAM + d_ * 4:V_LAM + (d_ + 1) * 4] = _pcols(f("rg_lam")[0, d_])
    vec[:, V_QG:V_QG + 3] = _pcols(f("mla_q_g")[0])
    vec[:, V_KVG:V_KVG + 2] = _pcols(f("mla_kv_g")[0])

    cp = _consts("prompt")
    cs = _consts("sample")
    xp, xs = f("x_prompt"), f("x_sample")
    in_maps = []
    for core in range(8):
        m = dict(shared)
        vv = vec.copy()
        if core < 4:
            xc = xp[4 * core:4 * core + 4].reshape(NT, D)
            cvv = f("c_ctx")
            m.update(cp)
            m["ckv_cacheT"] = np.zeros((256, 512), np.float32)
            m["kr_cacheT"] = np.zeros((32, 512), np.float32)
            vv[:, V_MASK] = 0.0
        else:
            b = core - 4
            xc = xs[b]
            cvv = f("c")[b]
            m.update(cs)
            m["ckv_cacheT"] = np.ascontiguousarray(f("cache_mla_ckv")[b, 0].T)
            m["kr_cacheT"] = np.ascontiguousarray(f("cache_mla_krope")[b, 0].T)
            st = f("state_rglru")[b, 0]
            for d_ in range(2):
                vv[:, V_ST + d_ * 4:V_ST + (d_ + 1) * 4] = _pcols(st[d_])
            vv[:, V_MASK] = 1.0
        m["xT"] = np.ascontiguousarray(xc.T)
        m["cv"] = _pcols(cvv)
        m["vecs"] = vv
        in_maps.append(m)
    return in_maps


def assemble(results):
    yp = np.zeros((16, 256, D), np.float32)
    ys = np.zeros((4, NT, D), np.float32)
    ckv = np.zeros((16, 1, 256, 256), np.float32)
    kr = np.zeros((16, 1, 256, 32), np.float32)
    st = np.zeros((16, 1, 2, 512), np.float32)
    for core in range(8):
        r = results[core]
        y = np.ascontiguousarray(r["yT"].T)
        if core < 4:
            yp[4 * core:4 * core + 4] = y.reshape(4, 256, D)
            ckv[4 * core:4 * core + 4, 0] = np.ascontiguousarray(r["ckvT"].T).reshape(4, 256, 256)
            kr[4 * core:4 * core + 4, 0] = np.ascontiguousarray(r["krT"].T).reshape(4, 256, 32)
            s_ = r["st"].reshape(128, 4, 2, 4)
            st[4 * core:4 * core + 4, 0] = s_.transpose(3, 2, 1, 0).reshape(4, 2, 512)
        else:
            ys[core - 4] = y
    return yp, ys, ckv, kr, st


_NC_CACHE = {}


def kernel(**inputs):
    in_maps = prep_inputs(inputs)
    if "nc" not in _NC_CACHE:
        _NC_CACHE["nc"] = build_program()
    res = run_bass_kernel_spmd(_NC_CACHE["nc"], in_maps, core_ids=list(range(8)))
    return assemble(res.results)
```

```python
import math
import numpy as np
import concourse.bass as bass
import concourse.mybir as mybir
from concourse.bass_utils import run_bass_kernel_spmd

F32 = mybir.dt.float32
BF16 = mybir.dt.bfloat16
U8 = mybir.dt.uint8
AF = mybir.ActivationFunctionType
ALU = mybir.AluOpType

D = 1024
NT = 1024
DEPTH = 2
ALPHA = (2 * DEPTH) ** 0.25
LN_EPS = 1e-5
RMS_EPS = 1e-6
D_FF = 2816
NJ = D_FF // 128
ATTN_SCALE = 1.0 / math.sqrt(96.0)
NEG = -512.0
GELU = AF.Gelu_apprx_tanh

V_BMOD = 0
V_LN = 96
V_PSC = 160
V_RGW = 164
V_RGB = 180
V_BA = 184
V_BX = 192
V_LAM = 200
V_QG = 208
V_KVG = 211
V_FCW = 216
V_FCB = 480
V_ST = 568
V_MASK = 576
NV = 580


class Tile:
    __slots__ = ("name", "ap", "p0", "p1", "b0", "b1", "w", "r", "live")


class Sched:
    def __init__(self, nc, arena, arena_bytes, n_dma_sems=56):
        self.nc = nc
        self.arena = arena
        self.arena_bytes = arena_bytes
        self.E = {"pe": nc.tensor, "act": nc.scalar, "dve": nc.vector, "pool": nc.gpsimd, "sp": nc.sync}
        self.sem = {k: nc.alloc_semaphore("sem_" + k) for k in ("pe", "act", "dve", "pool")}
        self.cnt = {k: 0 for k in self.sem}
        self.seen = {k: {} for k in self.E}
        self.dsem = [nc.alloc_semaphore("dsem%d" % i) for i in range(n_dma_sems)]
        self.dcnt = [0] * n_dma_sems
        self.ring = {"pool": list(range(0, 36)), "sp": list(range(36, n_dma_sems))}
        self.rpos = {"pool": 0, "sp": 0}
        self.tiles = []
        self.psum_tiles = []
        self.nwaits = 0
        self.pool_dmas = []
        self.MAX_SWDGE = 4

    def tile(self, name, off, cols, dtype, p0=0, p1=128):
        dsz = 4 if dtype == F32 else 2
        nbytes = cols * dsz
        assert off % 4 == 0 and off + nbytes <= self.arena_bytes, (name, off, nbytes)
        t = Tile()
        t.name = name
        t.ap = self.arena[p0:p1, off:off + nbytes].bitcast(dtype)
        t.p0, t.p1, t.b0, t.b1 = p0, p1, off, off + nbytes
        t.w = None
        t.r = {}
        t.live = True
        pend = []
        for o in self.tiles:
            if o.b0 < t.b1 and t.b0 < o.b1 and o.p0 < t.p1 and t.p0 < o.p1:
                o.live = False
                if o.w is not None:
                    pend.append(o.w)
                pend.extend(o.r.values())
        self.tiles.append(t)
        best = {}
        for ev in pend:
            if ev[2] > best.get(ev[1], (None, None, 0))[2]:
                best[ev[1]] = ev
        for k, ev in best.items():
            t.r[("inh", k)] = ev
        return t

    def psum(self, name, ap):
        t = Tile()
        t.name = name
        t.ap = ap
        t.w = None
        t.r = {}
        t.live = True
        t.p0, t.p1, t.b0, t.b1 = 0, 128, -1, -1
        return t

    def _wait(self, eng, evs):
        best = {}
        for (sem, key, val) in evs:
            if val > best.get(key, (None, 0))[1]:
                best[key] = (sem, val)
        for key, (sem, val) in best.items():
            if eng == "pe" and key == "pe":
                continue
            if self.seen[eng].get(key, 0) < val:
                self.E[eng].wait_ge(sem, val)
                self.seen[eng][key] = val
                self.nwaits += 1

    def _collect(self, r, w, eng=None):
        evs = []
        for t in r:
            assert t.live, "read of retired tile " + t.name
            if t.w is not None:
                evs.append(t.w)
            if t.b0 == -1:
                evs.extend(ev for k, ev in t.r.items() if k != eng)
        for t in w:
            assert t.live, "write of retired tile " + t.name
            if t.w is not None:
                evs.append(t.w)
            evs.extend(t.r.values())
        return evs

    def _commit(self, key, ev, r, w):
        for t in r:
            t.r[key] = ev
        for t in w:
            t.w = ev
            t.r = {}

    def op(self, eng, fn, r=(), w=(), pe_fence=None):
        self._wait(eng, self._collect(r, w, eng))
        inst = fn()
        if pe_fence is not None:
            inst = self.nc.tensor.matmul(pe_fence[0], pe_fence[1], pe_fence[1], start=True, stop=True)
        self.cnt[eng] += 1
        inst.then_inc(self.sem[eng], 1)
        ev = (self.sem[eng], eng, self.cnt[eng])
        self._commit(eng, ev, r, w)
        return ev

    def dma(self, q, out_ap, in_ap, r=(), w=()):
        evs = self._collect(r, w)
        ring = self.ring[q]
        i = ring[self.rpos[q] % len(ring)]
        self.rpos[q] += 1
        key = "d%d" % i
        if self.dcnt[i] > 0:
            evs.append((self.dsem[i], key, self.dcnt[i]))
        if q == "pool":
            if len(self.pool_dmas) >= self.MAX_SWDGE:
                evs.append(self.pool_dmas[-self.MAX_SWDGE])
        self._wait(q, evs)
        inst = self.E[q].dma_start(out=out_ap, in_=in_ap)
        self.dcnt[i] += 16
        inst.then_inc(self.dsem[i], 16)
        ev = (self.dsem[i], key, self.dcnt[i])
        if q == "pool":
            self.pool_dmas.append(ev)
        self._commit(key, ev, r, w)
        return ev

    def fence(self, eng, tiles, scratch):
        if eng == "act":
            fn = lambda: self.nc.scalar.activation(out=scratch.ap[:, 0:1], in_=scratch.ap[:, 1:2],
                                                   func=AF.Identity)
        elif eng == "dve":
            fn = lambda: self.nc.vector.tensor_copy(out=scratch.ap[:, 0:1], in_=scratch.ap[:, 1:2])
        else:
            fn = lambda: self.nc.gpsimd.tensor_copy(out=scratch.ap[:, 0:1], in_=scratch.ap[:, 1:2])
        inst = fn()
        self.cnt[eng] += 1
        inst.then_inc(self.sem[eng], 1)
        ev = (self.sem[eng], eng, self.cnt[eng])
        for t in tiles:
            t.w = ev

    def wait_all(self, eng, evs):
        self._wait(eng, evs)


def build_program(stop_after=None, debug=False, skip=()):
    nc = bass.Bass("TRN2", target_bir_lowering=False)

    def din(name, shape):
        return nc.dram_tensor(name, list(shape), F32, kind="ExternalInput").ap()

    def dout(name, shape):
        return nc.dram_tensor(name, list(shape), F32, kind="ExternalOutput").ap()

    d_xT = din("xT", [D, NT])
    d_cv = din("cv", [128, 8])
    d_vecs = din("vecs", [128, NV])
    d_wmod = din("w_mod", [2, D, 6 * D])
    d_ewin = din("even_w_in", [D, 1536])
    d_ewout = din("even_w_out", [D, D])
    d_wsT = din("w_sT", [128, 512])
    d_bs = din("b_s", [1, 512])
    d_wpool = din("w_pool", [128, 512])
    d_invcnt = din("invcnt", [128, 4096])
    d_owin_a = din("odd_w_in_a", [D, 1024])
    d_owin_b = din("odd_w_in_b", [D, 640])
    d_wkr = din("w_kr_pad", [D, 96])
    d_wkrs = din("w_kr_swap", [D, 96])
    d_wgate = din("w_gate", [128, 2048])
    d_wuq = din("w_uq", [384, 768])
    d_wuqs = din("w_uq_swap", [384, 768])
    d_wuk = din("w_uk", [256, 512])
    d_wuv = din("w_uv", [256, 512])
    d_owout = din("odd_w_out", [D, D])
    d_ckvc = din("ckv_cacheT", [256, 512])
    d_krc = din("kr_cacheT", [32, 512])
    d_maskq = din("maskq", [4, NT])
    d_maskk = din("maskk", [4, 1536])
    d_ropeC = din("ropeC", [32, NT])
    d_ropeS = din("ropeS", [32, NT])
    d_wup = din("ffn_w_up_r", [2, 6, D, 1024])
    d_wdown = din("ffn_w_down", [2, D_FF, D])
    o_yT = dout("yT", [D, NT])
    o_ckvT = dout("ckvT", [256, NT])
    o_krT = dout("krT", [32, NT])
    o_st = dout("st", [128, 32])
    o_dbg = dout("dbg", [128, 12288]) if debug else None
    o_dump = dout("dump", [16, 128, 1024]) if debug else None
    dump_state = {"n": 0, "names": []}

    ARENA = 206 * 1024
    arena = nc.alloc_sbuf_tensor("arena", [128, ARENA], U8)
    S = Sched(nc, arena, ARENA)
    T = S.tile
    out_events = []

    psum_all = nc.alloc_psum_tensor("psum_all", [128, 4096], F32)
    PB = [S.psum("pb%d" % i, psum_all[:, i * 512:(i + 1) * 512]) for i in range(8)]

    def pspan(i, n=2):
        return psum_all[:, i * 512:(i + n) * 512]

    off = 0
    xT = []
    for c in range(8):
        xT.append(T("xT%d" % c, off, NT, F32)); off += NT * 4
    hT = []
    for c in range(8):
        hT.append(T("hT%d" % c, off, NT, BF16)); off += NT * 2
    vecs = T("vecs", off, NV, F32); off += NV * 4
    cvt = T("cv", off, 8, F32); off += 32
    s_bf = T("s_bf", off, 8, BF16); off += 32
    modT = [T("modT0", off, 48, F32), T("modT1", off + 192, 48, F32)]; off += 384
    small = [T("small%d" % l, off + l * 256, 64, F32) for l in range(2)]; off += 512
    ones_bf = T("ones_bf", off, 128, BF16); off += 256
    epsln = T("epsln", off, 1, F32); off += 4
    epsrms = T("epsrms", off, 1, F32); off += 4
    nkeep = T("nkeep", off, 1, F32); off += 4
    off = (off + 63) // 64 * 64
    fixv = T("fixv", off, 2 * 2 * 44 + 16, F32); off += (2 * 2 * 44 + 16) * 4
    off = (off + 63) // 64 * 64
    fsc = {e: T("fsc_" + e, off + i * 8, 2, F32) for i, e in enumerate(("act", "dve", "pool"))}; off += 64
    for e_ in fsc.values():
        pass
    zo = T("zo", off, 128, BF16); off += 256
    one_f = T("one_f", off, 1, F32); off += 64
    st_t = T("st_t", off, 32, F32); off += 128
    clam = T("clam", off, 16, F32); off += 64
    off = (off + 63) // 64 * 64
    PERSIST_END = off

    v = vecs.ap

    def VC(col, n=1):
        return v[:, col:col + n]

    def dump(name, tile_, ap=None, p0=0, p1=128):
        if not debug:
            return
        i = dump_state["n"]
        dump_state["n"] += 1
        dump_state["names"].append(name)
        stg = T("stg%d" % i, ARENA - 4608, NT, F32)
        src = tile_.ap if ap is None else ap
        ncol = src.shape[-1]
        S.op("dve", lambda: nc.vector.tensor_copy(out=stg.ap[p0:p1, 0:ncol], in_=src), r=[tile_], w=[stg])
        out_events.append(S.dma("sp", o_dump[i, p0:p1, 0:ncol], stg.ap[p0:p1, 0:ncol], r=[stg]))
    build_program.dump_names = dump_state["names"]

    S.dma("sp", cvt.ap, d_cv, w=[cvt])
    S.dma("sp", vecs.ap, d_vecs, w=[vecs])
    for c in range(8):
        S.dma("sp", xT[c].ap, d_xT[c * 128:(c + 1) * 128, :], w=[xT[c]])
    S.op("pool", lambda: nc.gpsimd.memset(ones_bf.ap, 1.0), w=[ones_bf])
    for e_ in ("act", "dve", "pool"):
        S.op("pool", lambda: nc.gpsimd.memset(fsc[e_].ap, 0.0), w=[fsc[e_]])
    S.op("pool", lambda: nc.gpsimd.memset(one_f.ap, 1.0), w=[one_f])
    S.op("pool", lambda: nc.gpsimd.memset(zo.ap[:, 0:64], 0.0), w=[zo])
    S.op("pool", lambda: nc.gpsimd.memset(zo.ap[:, 64:128], 1.0), w=[zo])
    S.op("pool", lambda: nc.gpsimd.memset(epsln.ap, LN_EPS), w=[epsln])
    S.op("pool", lambda: nc.gpsimd.memset(epsrms.ap, RMS_EPS), w=[epsrms])
    S.op("dve", lambda: nc.vector.tensor_scalar(out=nkeep.ap, in0=VC(V_MASK), scalar1=-1.0, scalar2=None,
                                                 op0=ALU.add), r=[vecs], w=[nkeep])
    for l in range(2):
        for ti, tap in enumerate((0, 2)):
            S.op("dve", lambda l=l, ti=ti, tap=tap: nc.vector.tensor_scalar(
                out=fixv.ap[:, (l * 2 + ti) * 44:(l * 2 + ti + 1) * 44],
                in0=VC(V_FCW + (l * 3 + tap) * 44, 44), scalar1=nkeep.ap[:, 0:1], scalar2=None, op0=ALU.mult),
                r=[vecs, nkeep], w=[fixv])
    S.op("dve", lambda: nc.vector.tensor_scalar(out=fixv.ap[:, 176:192], in0=VC(V_RGW, 16),
                                                 scalar1=nkeep.ap[:, 0:1], scalar2=None, op0=ALU.mult),
         r=[vecs, nkeep], w=[fixv])
    S.op("act", lambda: nc.scalar.activation(out=s_bf.ap, in_=cvt.ap, func=AF.Silu), r=[cvt], w=[s_bf])

    def mod_setup(l, buf_off, W=512):
        assert W == 512
        npiece = 6144 // W
        nch = W // 128
        bufs = [T("wm%d_%d" % (l, i), buf_off + i * W * 16, W * 8, BF16) for i in range(2)]
        rowt = [T("wmrow%d_%d" % (l, i), buf_off + 2 * W * 16 + i * W * 4, W, F32, 0, 1) for i in range(2)]
        ps = PB[6]
        prow = PB[7]

        def load(k):
            b = bufs[k % 2]
            S.dma("pool", b.ap.rearrange("p (c n) -> p c n", c=8),
                  d_wmod[l, :, k * W:(k + 1) * W].rearrange("(c p) n -> p c n", p=128), w=[b])

        def transposes(k):
            rt = rowt[k % 2]

            def mm():
                last = None
                for n in range(nch):
                    col = k * nch + n
                    last = nc.tensor.matmul(ps.ap[:, col:col + 1], rt.ap[0:1, n * 128:(n + 1) * 128],
                                            one_f.ap[0:1, 0:1], start=True, stop=True)
                return last
            S.op("pe", mm, r=[rt, one_f], w=[ps])

        def piece(k):
            if k + 1 < npiece:
                load(k + 1)
            b = bufs[k % 2]
            bv = b.ap.rearrange("p (c n) -> p c n", c=8)

            def mm():
                last = None
                for kc in range(8):
                    last = nc.tensor.matmul(prow.ap[0:1, 0:W], s_bf.ap[:, kc:kc + 1], bv[:, kc, :],
                                            start=(kc == 0), stop=(kc == 7))
                return last
            S.op("pe", mm, r=[b, s_bf], w=[prow])
            rt = rowt[k % 2]
            S.op("dve", lambda: nc.vector.tensor_copy(out=rt.ap, in_=prow.ap[0:1, 0:W]), r=[prow], w=[rt])
            flush()
            pend_tr.append(k)
            if k == npiece - 1:
                flush()

        pend_tr = []

        def flush():
            while pend_tr:
                transposes(pend_tr.pop(0))

        def finish_a():
            flush()
            m = modT[l].ap
            sm = small[l].ap
            S.op("dve", lambda: nc.vector.tensor_tensor(out=m[:, 0:16], in0=ps.ap[:, 0:16],
                                                         in1=VC(V_BMOD + l * 48, 16), op=ALU.add),
                 r=[ps, vecs], w=[modT[l]])
            S.op("dve", lambda: nc.vector.tensor_scalar(out=sm[:, 0:8], in0=m[:, 8:16], scalar1=1.0, scalar2=None,
                                                         op0=ALU.add), r=[modT[l]], w=[small[l]])

        def finish_b():
            flush()
            m = modT[l].ap
            sm = small[l].ap
            S.op("dve", lambda: nc.vector.tensor_tensor(out=m[:, 16:48], in0=ps.ap[:, 16:48],
                                                         in1=VC(V_BMOD + l * 48 + 16, 32), op=ALU.add),
                 r=[ps, vecs], w=[modT[l]])
            S.op("dve", lambda: nc.vector.tensor_scalar(out=sm[:, 8:16], in0=m[:, 32:40], scalar1=1.0, scalar2=None,
                                                         op0=ALU.add), r=[modT[l]], w=[small[l]])
            S.op("dve", lambda: nc.vector.tensor_scalar(out=sm[:, 32:40], in0=VC(V_LN + l * 32 + 8, 8),
                                                         scalar1=ALPHA, scalar2=None, op0=ALU.mult),
                 r=[vecs], w=[small[l]])
            S.op("dve", lambda: nc.vector.tensor_scalar(out=sm[:, 40:48], in0=VC(V_LN + l * 32 + 24, 8),
                                                         scalar1=ALPHA, scalar2=None, op0=ALU.mult),
                 r=[vecs], w=[small[l]])
            S.op("dve", lambda: nc.vector.tensor_tensor(out=sm[:, 16:24], in0=VC(V_LN + l * 32 + 8, 8),
                                                         in1=sm[:, 8:16], op=ALU.mult),
                 r=[vecs, small[l]], w=[small[l]])
            S.op("dve", lambda: nc.vector.tensor_tensor(out=sm[:, 16:24], in0=sm[:, 16:24], in1=m[:, 24:32],
                                                         op=ALU.add), r=[modT[l], small[l]], w=[small[l]])

        def finish():
            finish_a()
            finish_b()
        finish.a = finish_a
        finish.b = finish_b
        load(0)
        return [lambda k=k: piece(k) for k in range(npiece)], finish

    def modulation(l, buf_off):
        pieces, finish = mod_setup(l, buf_off, 512)
        for p_ in pieces:
            p_()
        finish()

    def hb_next(l):
        sm = small[l].ap
        sn = small[l + 1].ap
        S.op("dve", lambda: nc.vector.tensor_tensor(out=sm[:, 24:32], in0=VC(V_LN + l * 32 + 24, 8), in1=sn[:, 0:8],
                                                     op=ALU.mult), r=[vecs, small[l], small[l + 1]], w=[small[l]])
        S.op("dve", lambda: nc.vector.tensor_tensor(out=sm[:, 24:32], in0=sm[:, 24:32], in1=modT[l + 1].ap[:, 0:8],
                                                     op=ALU.add), r=[modT[l + 1], small[l]], w=[small[l]])

    MIX_BASE = PERSIST_END
    mod0_pieces, mod0_finish = mod_setup(0, MIX_BASE + 129088, 512)
    for p_ in mod0_pieces[:4]:
        p_()
    mod0_finish.a()

    for c in range(8):
        S.op("act", lambda c=c: nc.scalar.activation(out=hT[c].ap, in_=xT[c].ap, func=AF.Identity,
                                                     bias=modT[0].ap[:, c:c + 1], scale=small[0].ap[:, c:c + 1]),
             r=[xT[c], modT[0], small[0]], w=[hT[c]])
        S.op("dve", lambda c=c: nc.vector.tensor_scalar(out=xT[c].ap, in0=xT[c].ap, scalar1=ALPHA, scalar2=None,
                                                         op0=ALU.mult), r=[xT[c]], w=[xT[c]])

    def proj_fm(w_view, ncols0, hsrc, psb, kch=8):
        def mm():
            last = None
            for th in range(2):
                for kc in range(kch):
                    last = nc.tensor.matmul(PB[psb + th].ap, w_view[:, kc, ncols0:ncols0 + 128],
                                            hsrc[kc].ap[:, th * 512:(th + 1) * 512],
                                            start=(kc == 0), stop=(kc == kch - 1))
            return last
        return mm

    def proj_fine(w_view, w_tile, ncols0, hsrc, psb, kch=8):
        for kc in range(kch):
            def mm(kc=kc):
                last = None
                for th in range(2):
                    last = nc.tensor.matmul(PB[psb + th].ap, w_view[:, kc, ncols0:ncols0 + 128],
                                            hsrc[kc].ap[:, th * 512:(th + 1) * 512],
                                            start=(kc == 0), stop=(kc == kch - 1))
                return last
            S.op("pe", mm, r=[w_tile, hsrc[kc]], w=[PB[psb], PB[psb + 1]])

    def layernorm(l, which, base, final=False):
        g_col = V_LN + l * 32 + (0 if which == 1 else 16)
        b_col = g_col + 8
        o = base
        ybf = [T("ybf%d" % i, o + i * 2048, NT, BF16) for i in range(2)]; o += 4096
        ysq = [T("ysq%d" % i, o + i * 2048, NT, BF16) for i in range(2)]; o += 4096
        msq = T("msq", o, NT, F32); o += 4096
        rstd = T("rstd", o, NT, F32); o += 4096
        nmr = T("nmr", o, NT, F32); o += 4096
        for c in range(8):
            yb, ys = ybf[c % 2], ysq[c % 2]
            S.op("dve", lambda c=c, yb=yb: nc.vector.tensor_copy(out=yb.ap, in_=xT[c].ap), r=[xT[c]], w=[yb])
            S.op("act", lambda c=c, ys=ys: nc.scalar.activation(out=ys.ap, in_=xT[c].ap, func=AF.Square),
                 r=[xT[c]], w=[ys])

            def mm(c=c, yb=yb, ys=ys):
                last = None
                for th in range(2):
                    nc.tensor.matmul(PB[th].ap, ones_bf.ap, yb.ap[:, th * 512:(th + 1) * 512],
                                     start=(c == 0), stop=(c == 7))
                    last = nc.tensor.matmul(PB[2 + th].ap, ones_bf.ap, ys.ap[:, th * 512:(th + 1) * 512],
                                            start=(c == 0), stop=(c == 7))
                return last
            S.op("pe", mm, r=[yb, ys, ones_bf], w=[PB[0], PB[1], PB[2], PB[3]])
        S1 = pspan(0)
        S2 = pspan(2)
        S.op("act", lambda: nc.scalar.activation(out=msq.ap, in_=S1, func=AF.Square, scale=1.0 / D),
             r=[PB[0], PB[1]], w=[msq])
        S.op("dve", lambda: nc.vector.scalar_tensor_tensor(out=rstd.ap, in0=S2, scalar=1.0 / D, in1=msq.ap,
                                                            op0=ALU.mult, op1=ALU.subtract),
             r=[PB[2], PB[3], msq], w=[rstd])
        S.op("act", lambda: nc.scalar.activation(out=rstd.ap, in_=rstd.ap, func=AF.Sqrt, bias=epsln.ap[:, 0:1]),
             r=[rstd, epsln], w=[rstd])
        S.op("dve", lambda: nc.vector.reciprocal(out=rstd.ap, in_=rstd.ap), r=[rstd], w=[rstd])
        S.op("dve", lambda: nc.vector.scalar_tensor_tensor(out=nmr.ap, in0=S1, scalar=-1.0 / D, in1=rstd.ap,
                                                            op0=ALU.mult, op1=ALU.mult),
             r=[PB[0], PB[1], rstd], w=[nmr])
        sm = small[l].ap
        for c in range(8):
            S.op("dve", lambda c=c: nc.vector.scalar_tensor_tensor(out=xT[c].ap, in0=xT[c].ap,
                                                                    scalar=VC(g_col + c), in1=rstd.ap,
                                                                    op0=ALU.mult, op1=ALU.mult),
                 r=[xT[c], vecs, rstd], w=[xT[c]])
            S.op("dve", lambda c=c: nc.vector.scalar_tensor_tensor(out=xT[c].ap, in0=nmr.ap,
                                                                    scalar=VC(g_col + c), in1=xT[c].ap,
                                                                    op0=ALU.mult, op1=ALU.add),
                 r=[xT[c], vecs, nmr], w=[xT[c]])
            if final:
                S.op("act", lambda c=c: nc.scalar.activation(out=xT[c].ap, in_=xT[c].ap, func=AF.Identity,
                                                             bias=VC(b_col + c)),
                     r=[xT[c], vecs], w=[xT[c]])
                out_events.append(S.dma("sp", o_yT[c * 128:(c + 1) * 128, :], xT[c].ap, r=[xT[c]]))
            else:
                if which == 1:
                    sc = sm[:, 8 + c:9 + c]; hb = sm[:, 16 + c:17 + c]; ab = sm[:, 32 + c:33 + c]
                else:
                    sc = small[l + 1].ap[:, c:c + 1]; hb = sm[:, 24 + c:25 + c]; ab = sm[:, 40 + c:41 + c]
                rr = [xT[c], small[l]] + ([small[l + 1]] if which == 2 else [])
                S.op("act", lambda c=c, sc=sc, hb=hb: nc.scalar.activation(out=hT[c].ap, in_=xT[c].ap,
                                                                            func=AF.Identity, bias=hb, scale=sc),
                     r=rr, w=[hT[c]])
                S.op("pool", lambda c=c, ab=ab: nc.gpsimd.tensor_scalar(out=xT[c].ap, in0=xT[c].ap, scalar1=ALPHA,
                                                                         scalar2=ab, op0=ALU.mult, op1=ALU.add),
                     r=[xT[c], small[l]], w=[xT[c]])

    def residual_evac(l, gate_col0, nch, psb):
        S.op("dve", lambda: nc.vector.scalar_tensor_tensor(out=xT[nch].ap, in0=pspan(psb),
                                                            scalar=modT[l].ap[:, gate_col0 + nch:gate_col0 + nch + 1],
                                                            in1=xT[nch].ap, op0=ALU.mult, op1=ALU.add),
             r=[PB[psb], PB[psb + 1], modT[l], xT[nch]], w=[xT[nch]])

    def even_mixer(l, base, hook_setup=None, pre_hooks=None, pre_finish=None):
        o = base
        w_inA = T("ew_inA", o, 8192, BF16); o += 16384
        w_inB = T("ew_inB", o, 4096, BF16); o += 8192
        w_out = T("ew_out", o, 8192, BF16); o += 16384
        wsT = T("wsT", o, 512, BF16); o += 1024
        wpool = T("wpool", o, 512, BF16); o += 1024
        bsf = T("bsf", o, 512, F32, 0, 1); o += 2048
        bshi = T("bshi", o, 512, BF16, 0, 1); o += 1024
        bslo = T("bslo", o, 512, BF16, 0, 1); o += 1024
        bstmp = T("bstmp", o, 512, F32, 0, 1); o += 2048
        invc = T("invcnt", o, 4096, F32); o += 16384
        uT = [T("uT%d" % i, o + i * 4096, NT, F32) for i in range(4)]; o += 16384
        vn = [T("vn%d" % i, o + i * 1024, 512, BF16) for i in range(8)]; o += 8192
        gv = [T("gv%d" % i, o + i * 2048, 512, F32) for i in range(2)]; o += 4096
        st6 = [T("st6_%d" % i, o + i * 32, 8, F32) for i in range(2)]; o += 64
        PW = 4 * 272
        Ph = [T("Ph%d" % i, o + i * PW * 4, PW, F32) for i in range(4)]; o += 4 * PW * 4
        Sa = [T("Sa%d" % i, o + i * PW * 4, PW, F32) for i in range(2)]; o += 2 * PW * 4
        Sb = [T("Sb%d" % i, o + i * PW * 4, PW, F32) for i in range(2)]; o += 2 * PW * 4
        o_pooled = o; o += 8192
        o_yT = o; o += 16384
        END = o
        pre = list(pre_hooks) if pre_hooks else []

        def run_pre():
            if pre:
                pre.pop(0)()

        S.dma("pool", w_inA.ap.rearrange("p (c n) -> p c n", c=8),
              d_ewin[:, 0:1024].rearrange("(c p) n -> p c n", p=128), w=[w_inA])
        S.dma("pool", w_inB.ap.rearrange("p (c n) -> p c n", c=8),
              d_ewin[:, 1024:1536].rearrange("(c p) n -> p c n", p=128), w=[w_inB])
        S.dma("pool", wsT.ap, d_wsT, w=[wsT])
        S.dma("pool", wpool.ap, d_wpool, w=[wpool])
        S.dma("sp", bsf.ap, d_bs, w=[bsf])
        S.dma("sp", invc.ap, d_invcnt, w=[invc])
        S.dma("pool", w_out.ap.rearrange("p (c n) -> p c n", c=8),
              d_ewout.rearrange("(c p) n -> p c n", p=128), w=[w_out])
        S.op("dve", lambda: nc.vector.tensor_copy(out=bshi.ap, in_=bsf.ap), r=[bsf], w=[bshi])
        S.op("dve", lambda: nc.vector.tensor_copy(out=bstmp.ap, in_=bshi.ap), r=[bshi], w=[bstmp])
        S.op("dve", lambda: nc.vector.tensor_tensor(out=bslo.ap, in0=bsf.ap, in1=bstmp.ap, op=ALU.subtract),
             r=[bsf, bstmp], w=[bslo])
        for t_ in Ph:
            S.op("pool", lambda t_=t_: nc.gpsimd.memset(t_.ap, 0.0), w=[t_])

        wA = w_inA.ap.rearrange("p (c n) -> p c n", c=8)
        wB = w_inB.ap.rearrange("p (c n) -> p c n", c=8)
        wO = w_out.ap.rearrange("p (c n) -> p c n", c=8)
        for n in range(4):
            psb = (n % 2) * 2
            S.op("pe", proj_fm(wA, n * 128, hT, psb), r=[w_inA] + hT, w=[PB[psb], PB[psb + 1]])
            S.op("act", lambda n=n, psb=psb: nc.scalar.activation(out=uT[n].ap, in_=pspan(psb), func=GELU),
                 r=[PB[psb], PB[psb + 1]], w=[uT[n]])
            run_pre()
        for g in range(4):
            psb = 4
            S.op("pe", proj_fm(wB, g * 128, hT, psb), r=[w_inB] + hT, w=[PB[psb], PB[psb + 1]])
            dst = Ph[g].ap.rearrange("p (s x) -> p s x", s=4)[:, :, 8:264]
            S.op("act", lambda g=g, psb=psb, dst=dst: nc.scalar.activation(
                out=dst, in_=pspan(psb).rearrange("p (s x) -> p s x", s=4), func=AF.Identity),
                r=[PB[psb], PB[psb + 1]], w=[Ph[g]])
            run_pre()
        for tc in range(8):
            psb = tc % 4
            def mmv(tc=tc, psb=psb):
                last = None
                for kc in range(8):
                    last = nc.tensor.matmul(PB[psb].ap, hT[kc].ap[:, tc * 128:(tc + 1) * 128], wA[:, kc, 512:1024],
                                            start=(kc == 0), stop=(kc == 7))
                return last
            S.op("pe", mmv, r=[w_inA] + hT, w=[PB[psb]])
            g_ = gv[tc % 2]
            s6 = st6[tc % 2]
            S.op("act", lambda g_=g_, psb=psb: nc.scalar.activation(out=g_.ap, in_=PB[psb].ap, func=GELU),
                 r=[PB[psb]], w=[g_])
            S.op("dve", lambda g_=g_, s6=s6: nc.vector.bn_stats(out=s6.ap[:, 0:6], in_=g_.ap), r=[g_], w=[s6])
            S.op("dve", lambda s6=s6: nc.vector.bn_aggr(out=s6.ap[:, 6:8], in_=s6.ap[:, 0:6]), r=[s6], w=[s6])
            S.op("act", lambda s6=s6: nc.scalar.activation(out=s6.ap[:, 7:8], in_=s6.ap[:, 7:8], func=AF.Sqrt,
                                                           bias=epsln.ap[:, 0:1]), r=[s6, epsln], w=[s6])
            S.op("dve", lambda s6=s6: nc.vector.reciprocal(out=s6.ap[:, 7:8], in_=s6.ap[:, 7:8]), r=[s6], w=[s6])
            S.op("dve", lambda g_=g_, s6=s6, tc=tc: nc.vector.tensor_scalar(
                out=vn[tc].ap, in0=g_.ap, scalar1=s6.ap[:, 6:7], scalar2=s6.ap[:, 7:8],
                op0=ALU.subtract, op1=ALU.mult), r=[g_, s6], w=[vn[tc]])
            run_pre()
        while pre:
            run_pre()
        if pre_finish is not None:
            pre_finish()
        pooled = [T("pooled%d" % i, o_pooled + i * 2048, NT, BF16) for i in range(4)]
        yT = [T("yT%d" % i, o_yT + i * 2048, NT, BF16) for i in range(8)]
        hooks, hook_finish = ([], None)
        if hook_setup is not None:
            hooks, hook_finish = hook_setup(w_inA.b0)

        def run_hook():
            if hooks:
                hooks.pop(0)()
        for h in range(4):
            psb = (h % 2) * 2
            def mmg(h=h, psb=psb):
                last = None
                for n in range(8):
                    o_ = PB[psb + n // 4].ap[:, (n % 4) * 128:(n % 4 + 1) * 128]
                    nc.tensor.matmul(o_, vn[n].ap[:, h * 128:(h + 1) * 128], wsT.ap[:, h * 128:(h + 1) * 128],
                                     start=True, stop=False)
                    nc.tensor.matmul(o_, ones_bf.ap[0:1, :], bshi.ap[0:1, h * 128:(h + 1) * 128],
                                     start=False, stop=False)
                    last = nc.tensor.matmul(o_, ones_bf.ap[0:1, :], bslo.ap[0:1, h * 128:(h + 1) * 128],
                                            start=False, stop=True)
                return last
            S.op("pe", mmg, r=vn + [wsT, ones_bf, bshi, bslo], w=[PB[psb], PB[psb + 1]])
            S.op("dve", lambda h=h, psb=psb: nc.vector.tensor_tensor(out=yT[h].ap, in0=pspan(psb), in1=uT[h].ap,
                                                                      op=ALU.mult),
                 r=[PB[psb], PB[psb + 1], uT[h]], w=[yT[h]])
            run_hook()
        for g in range(4):
            w_ = (2, 4, 8, 16)[g]
            P3 = Ph[g].ap.rearrange("p (s x) -> p s x", s=4)
            eng = "pool" if g % 2 == 0 else "dve"
            EN = S.E[eng]
            S.op(eng, lambda P3=P3, EN=EN: EN.tensor_scalar(out=P3[:, 1:4, 0:8], in0=P3[:, 0:3, 256:264],
                                                            scalar1=VC(V_MASK), scalar2=None, op0=ALU.mult),
                 r=[Ph[g], vecs], w=[Ph[g]])
            S.op(eng, lambda P3=P3, EN=EN: EN.tensor_scalar(out=P3[:, 0:3, 264:272], in0=P3[:, 1:4, 8:16],
                                                            scalar1=VC(V_MASK), scalar2=None, op0=ALU.mult),
                 r=[Ph[g], vecs], w=[Ph[g]])
            A3 = Sa[g % 2].ap.rearrange("p (s x) -> p s x", s=4)
            B3 = Sb[g % 2].ap.rearrange("p (s x) -> p s x", s=4)
            ta, tb = Sa[g % 2], Sb[g % 2]
            S.op(eng, lambda: EN.tensor_tensor(out=A3[:, :, 0:271], in0=P3[:, :, 0:271], in1=P3[:, :, 1:272],
                                               op=ALU.add), r=[Ph[g]], w=[ta])
            cur, curt, oth, otht, ext = A3, ta, B3, tb, 271
            step = 2
            while step < w_:
                e2 = ext - step
                S.op(eng, lambda cur=cur, oth=oth, e2=e2, step=step: EN.tensor_tensor(
                    out=oth[:, :, 0:e2], in0=cur[:, :, 0:e2], in1=cur[:, :, step:step + e2], op=ALU.add),
                    r=[curt], w=[otht])
                cur, curt, oth, otht, ext = oth, otht, cur, curt, e2
                step *= 2
            s0 = 8 - w_ // 2
            ic = invc.ap[:, g * 1024:(g + 1) * 1024].rearrange("p (s x) -> p s x", s=4)
            S.op(eng, lambda cur=cur, oth=oth, s0=s0, ic=ic: EN.tensor_tensor(
                out=oth[:, :, 0:256], in0=cur[:, :, s0:s0 + 256], in1=ic, op=ALU.mult), r=[curt, invc], w=[otht])
            pl = pooled[g].ap.rearrange("p (s x) -> p s x", s=4)
            S.op(eng, lambda oth=oth, pl=pl, P3=P3: EN.tensor_tensor(out=pl, in0=oth[:, :, 0:256],
                                                                      in1=P3[:, :, 8:264], op=ALU.subtract),
                 r=[otht, Ph[g]], w=[pooled[g]])
            psb = 4
            def mmp(g=g, psb=psb):
                last = None
                for th in range(2):
                    last = nc.tensor.matmul(PB[psb + th].ap, wpool.ap[:, g * 128:(g + 1) * 128],
                                            pooled[g].ap[:, th * 512:(th + 1) * 512], start=True, stop=True)
                return last
            S.op("pe", mmp, r=[wpool, pooled[g]], w=[PB[psb], PB[psb + 1]])
            S.op("act", lambda g=g, psb=psb: nc.scalar.activation(out=yT[4 + g].ap, in_=pspan(psb), func=AF.Identity,
                                                                  scale=VC(V_PSC + g)),
                 r=[PB[psb], PB[psb + 1], vecs], w=[yT[4 + g]])
            run_hook()
        for n in range(8):
            psb = (0, 2, 4)[n % 3]
            S.op("pe", proj_fm(wO, n * 128, yT, psb), r=[w_out] + yT, w=[PB[psb], PB[psb + 1]])
            residual_evac(l, 16, n, psb)
            run_hook()
        while hooks:
            run_hook()
        if hook_finish is not None:
            hook_finish()
        return END

    def ffn_prefetch(l, slab_base):
        slabs = [T("slab%d_%d" % (l, i), slab_base + i * 16384, 8192, BF16) for i in range(2)]
        for s_ in range(2):
            b = slabs[s_ % 2]
            S.dma("pool", b.ap.rearrange("p (c n) -> p c n", c=8)[:, :, 0:1024],
                  d_wup[l, s_, :, 0:1024].rearrange("(c p) n -> p c n", p=128), w=[b])
        return slabs

    def ffn(l, base, slab_base, slabs, pre_down=None):
        o = base
        aT = [T("aT%d" % j, o + j * 2048, NT, BF16) for j in range(NJ)]; o += NJ * 2048
        wdn = [T("wdn%d" % i, o + i * 4096, 2048, BF16) for i in range(NJ // 2)]; o += NJ * 2048
        o = slab_base + 32768
        tmp = []
        for i in range(2):
            tmp.append((T("accg%d" % i, o, NT, F32), T("accv%d" % i, o + 4096, NT, F32),
                        T("sg%d" % i, o + 8192, NT, F32)))
            o += 12288
        END = o
        def wdv(j):
            return wdn[j // 2].ap.rearrange("p (j n) -> p j n", j=2)[:, j % 2, :]

        def load_slab(s):
            b = slabs[s % 2]
            ncol = 1024 if s < 5 else 512
            S.dma("pool", b.ap.rearrange("p (c n) -> p c n", c=8)[:, :, 0:ncol],
                  d_wup[l, s, :, 0:ncol].rearrange("(c p) n -> p c n", p=128), w=[b])
        for j in range(NJ):
            s, jj = j // 4, j % 4
            half = 512 if s < 5 else 256
            sv = slabs[s % 2].ap.rearrange("p (c n) -> p c n", c=8)
            if 2 <= j < 2 + NJ // 2:
                q4 = (j - 2) * 2
                S.dma("pool", wdn[q4 // 2].ap.rearrange("p (j n) -> p j n", j=2),
                      d_wdown[l, q4 * 128:(q4 + 2) * 128, :].rearrange("(j p) n -> p j n", p=128),
                      w=[wdn[q4 // 2]])
            pg = (j % 2) * 4
            if j == 0:
                proj_fine(sv, slabs[0], jj * 128, hT, pg)
            else:
                S.op("pe", proj_fm(sv, jj * 128, hT, pg), r=[slabs[s % 2]] + hT, w=[PB[pg], PB[pg + 1]])
            S.op("pe", proj_fm(sv, half + jj * 128, hT, pg + 2), r=[slabs[s % 2]] + hT, w=[PB[pg + 2], PB[pg + 3]])
            if jj == 3 and s + 2 < 6:
                load_slab(s + 2)
            accg, accv, sg = tmp[j % 2]
            for which, acc, pb0 in ((0, accg, pg), (1, accv, pg + 2)):
                ch = which * NJ + j
                z = pspan(pb0)
                cw0 = VC(V_FCW + (l * 3 + 0) * 44 + ch)
                cw1 = VC(V_FCW + (l * 3 + 1) * 44 + ch)
                cw2 = VC(V_FCW + (l * 3 + 2) * 44 + ch)
                cb = VC(V_FCB + l * 44 + ch)
                f0 = fixv.ap[:, (l * 2 + 0) * 44 + ch:(l * 2 + 0) * 44 + ch + 1]
                f2 = fixv.ap[:, (l * 2 + 1) * 44 + ch:(l * 2 + 1) * 44 + ch + 1]
                rp = [PB[pb0], PB[pb0 + 1]]
                S.op("act", lambda acc=acc, z=z, cw1=cw1, cb=cb: nc.scalar.activation(
                    out=acc.ap, in_=z, func=AF.Identity, bias=cb, scale=cw1), r=rp + [vecs], w=[acc])
                S.op("dve", lambda acc=acc, z=z, cw0=cw0: nc.vector.scalar_tensor_tensor(
                    out=acc.ap[:, 1:NT], in0=z[:, 0:NT - 1], scalar=cw0, in1=acc.ap[:, 1:NT],
                    op0=ALU.mult, op1=ALU.add), r=rp + [vecs, acc], w=[acc])
                S.op("dve", lambda acc=acc, z=z, cw2=cw2: nc.vector.scalar_tensor_tensor(
                    out=acc.ap[:, 0:NT - 1], in0=z[:, 1:NT], scalar=cw2, in1=acc.ap[:, 0:NT - 1],
                    op0=ALU.mult, op1=ALU.add), r=rp + [vecs, acc], w=[acc])
                S.op("dve", lambda acc=acc, z=z, f0=f0: nc.vector.scalar_tensor_tensor(
                    out=acc.ap[:, 256:NT:256], in0=z[:, 255:NT - 1:256], scalar=f0, in1=acc.ap[:, 256:NT:256],
                    op0=ALU.mult, op1=ALU.add), r=rp + [fixv, acc], w=[acc])
                S.op("dve", lambda acc=acc, z=z, f2=f2: nc.vector.scalar_tensor_tensor(
                    out=acc.ap[:, 255:NT - 1:256], in0=z[:, 256:NT:256], scalar=f2, in1=acc.ap[:, 255:NT - 1:256],
                    op0=ALU.mult, op1=ALU.add), r=rp + [fixv, acc], w=[acc])
            S.op("act", lambda accg=accg, sg=sg: nc.scalar.activation(out=sg.ap, in_=accg.ap, func=AF.Silu),
                 r=[accg], w=[sg])
            S.op("pool", lambda sg=sg, accv=accv, j=j: nc.gpsimd.tensor_tensor(out=aT[j].ap, in0=sg.ap, in1=accv.ap,
                                                                                op=ALU.mult),
                 r=[sg, accv], w=[aT[j]])
        if pre_down is not None:
            pre_down()
        for n in range(8):
            psb = (n % 2) * 2
            def mmd(n=n, psb=psb):
                last = None
                for th in range(2):
                    for j in range(NJ):
                        last = nc.tensor.matmul(PB[psb + th].ap, wdv(j)[:, n * 128:(n + 1) * 128],
                                                aT[j].ap[:, th * 512:(th + 1) * 512],
                                                start=(j == 0), stop=(j == NJ - 1))
                return last
            S.op("pe", mmd, r=wdn + aT, w=[PB[psb], PB[psb + 1]])
            residual_evac(l, 40, n, psb)
        return END

    def odd_mixer(l, base):
        M = base
        w_inA = T("ow_inA", M + 0, 8192, BF16)
        wgate = T("wgate", M + 16384, 2048, BF16)
        o = M + 20480
        xcs = [T("xc%d" % i, o + i * 4096, NT, F32) for i in range(2)]; o += 8192
        xcb = T("xcb", o, NT, BF16); o += 2048
        ggr = T("ggr", o, NT, F32); o += 4096
        rgt = []
        for d_ in range(2):
            tl_ = []
            for k in range(5):
                if d_ == 1 and k >= 3:
                    tl_.append(T("rg%d_%d" % (d_, k), M + 142336 + (k - 3) * 4096, NT, F32))
                else:
                    tl_.append(T("rg%d_%d" % (d_, k), o, NT, F32)); o += 4096
            rgt.append(tl_)
        assert o <= M + 67584
        ycT = [T("ycT%d" % c, M + 67584 + c * 2048, NT, BF16) for c in range(4)]
        w_inB = T("ow_inB", M + 75776, 5120, BF16)
        wkr = T("wkr", M + 86016, 768, BF16)
        wkrs = T("wkrs", M + 87552, 768, BF16)
        wuq = T("wuq", M + 89088, 2304, BF16)
        wuqs = T("wuqs", M + 93696, 2304, BF16)
        wuk = T("wuk", M + 98304, 1024, BF16)
        wuv = T("wuv", M + 100352, 1024, BF16)
        w_out = T("ow_out", M + 102400, 8192, BF16)
        ropeC = T("ropeC", M + 126976, NT, F32, 64, 96)
        ropeS = T("ropeS", M + 131072, NT, F32, 64, 96)
        KRall = T("KRall", M + 135168, 1536, BF16, 64, 96)
        krout = T("krout", M + 138240, NT, F32, 64, 96)
        END = M + 150528

        def cast_rows(dst_tile, src, kc, ncol):
            if dst_tile.name in skip:
                return
            S.dma("pool", dst_tile.ap.rearrange("p (c n) -> p c n", c=kc),
                  src.rearrange("(c p) n -> p c n", p=128), w=[dst_tile])
        cast_rows(w_inA, d_owin_a, 8, 1024)
        for i in range(2):
            S.dma("pool", wgate.ap[:, i * 1024:(i + 1) * 1024], d_wgate[:, i * 1024:(i + 1) * 1024], w=[wgate])
        cast_rows(w_inB, d_owin_b, 8, 640)
        cast_rows(wkr, d_wkr, 8, 96)
        cast_rows(wkrs, d_wkrs, 8, 96)
        cast_rows(wuq, d_wuq, 3, 768)
        cast_rows(wuqs, d_wuqs, 3, 768)
        cast_rows(wuk, d_wuk, 2, 512)
        cast_rows(wuv, d_wuv, 2, 512)
        cast_rows(w_out, d_owout, 8, 1024)
        S.dma("sp", ropeC.ap, d_ropeC, w=[ropeC])
        S.dma("sp", ropeS.ap, d_ropeS, w=[ropeS])
        S.dma("sp", krout.ap[:, 0:512], d_krc, w=[krout])
        S.op("pool", lambda: nc.gpsimd.tensor_copy(out=KRall.ap[:, 0:512], in_=krout.ap[:, 0:512]),
             r=[krout], w=[KRall])

        S.op("act", lambda: nc.scalar.activation(out=clam.ap[:, 0:8], in_=VC(V_LAM, 8), func=AF.Exp, scale=-1.0),
             r=[vecs], w=[clam])
        S.op("act", lambda: nc.scalar.activation(out=clam.ap[:, 0:8], in_=clam.ap[:, 0:8], func=AF.Ln, bias=1.0),
             r=[clam], w=[clam])
        S.op("dve", lambda: nc.vector.tensor_scalar(out=clam.ap[:, 8:16], in0=clam.ap[:, 0:8], scalar1=-16.0,
                                                     scalar2=None, op0=ALU.mult), r=[clam], w=[clam])
        S.op("dve", lambda: nc.vector.tensor_scalar(out=clam.ap[:, 0:8], in0=clam.ap[:, 0:8], scalar1=-8.0,
                                                     scalar2=None, op0=ALU.mult), r=[clam], w=[clam])

        if debug == "canary":
            out_events.append(S.dma("sp", o_dbg[:, 4400:4400 + NV], vecs.ap, r=[vecs]))
            S.wait_all("sp", out_events)
            return END
        if debug == "hT":
            for c in range(4):
                S.op("dve", lambda: nc.vector.tensor_copy(out=xc.ap, in_=hT[c].ap), r=[hT[c]], w=[xc])
                out_events.append(S.dma("sp", o_dbg[:, c * 1024:(c + 1) * 1024], xc.ap, r=[xc]))
            out_events.append(S.dma("sp", o_dbg[:, 4096:4096 + 48], modT[1].ap, r=[modT[1]]))
            out_events.append(S.dma("sp", o_dbg[:, 4200:4264], small[0].ap, r=[small[0]]))
            out_events.append(S.dma("sp", o_dbg[:, 4300:4364], small[1].ap, r=[small[1]]))
            out_events.append(S.dma("sp", o_dbg[:, 4400:4400 + NV], vecs.ap, r=[vecs]))
            out_events.append(S.dma("sp", o_dbg[:, 5000:5016], clam.ap, r=[clam]))
            for c in range(4):
                out_events.append(S.dma("sp", o_dbg[:, 6000 + c * 1024:6000 + (c + 1) * 1024], xT[c].ap, r=[xT[c]]))
        wA = w_inA.ap.rearrange("p (c n) -> p c n", c=8)
        wB = w_inB.ap.rearrange("p (c n) -> p c n", c=8)
        for c in range(4):
            xc = xcs[c % 2]
            if c == 0:
                proj_fine(wA, w_inA, 0, hT, 0)
            else:
                S.op("pe", proj_fm(wA, c * 128, hT, 0), r=[w_inA] + hT, w=[PB[0], PB[1]])
            z = pspan(0)
            rp = [PB[0], PB[1]]
            cw = [VC(V_RGW + tap * 4 + c) for tap in range(4)]
            fx = [fixv.ap[:, 176 + tap * 4 + c:177 + tap * 4 + c] for tap in range(4)]
            S.op("act", lambda: nc.scalar.activation(out=xc.ap, in_=z, func=AF.Identity, bias=VC(V_RGB + c),
                                                     scale=cw[1]), r=rp + [vecs], w=[xc])
            S.op("dve", lambda: nc.vector.scalar_tensor_tensor(out=xc.ap[:, 1:NT], in0=z[:, 0:NT - 1], scalar=cw[0],
                                                                in1=xc.ap[:, 1:NT], op0=ALU.mult, op1=ALU.add),
                 r=rp + [vecs, xc], w=[xc])
            S.op("dve", lambda: nc.vector.scalar_tensor_tensor(out=xc.ap[:, 0:NT - 1], in0=z[:, 1:NT], scalar=cw[2],
                                                                in1=xc.ap[:, 0:NT - 1], op0=ALU.mult, op1=ALU.add),
                 r=rp + [vecs, xc], w=[xc])
            S.op("dve", lambda: nc.vector.scalar_tensor_tensor(out=xc.ap[:, 0:NT - 2], in0=z[:, 2:NT], scalar=cw[3],
                                                                in1=xc.ap[:, 0:NT - 2], op0=ALU.mult, op1=ALU.add),
                 r=rp + [vecs, xc], w=[xc])
            S.op("dve", lambda: nc.vector.scalar_tensor_tensor(out=xc.ap[:, 256:NT:256], in0=z[:, 255:NT - 1:256],
                                                                scalar=fx[0], in1=xc.ap[:, 256:NT:256],
                                                                op0=ALU.mult, op1=ALU.add), r=rp + [fixv, xc], w=[xc])
            S.op("dve", lambda: nc.vector.scalar_tensor_tensor(out=xc.ap[:, 255:NT - 1:256], in0=z[:, 256:NT:256],
                                                                scalar=fx[2], in1=xc.ap[:, 255:NT - 1:256],
                                                                op0=ALU.mult, op1=ALU.add), r=rp + [fixv, xc], w=[xc])
            x3 = xc.ap.rearrange("p (s x) -> p s x", s=4)
            z3 = z.rearrange("p (s x) -> p s x", s=4)
            S.op("dve", lambda: nc.vector.scalar_tensor_tensor(out=x3[:, 0:3, 254:256], in0=z3[:, 1:4, 0:2],
                                                                scalar=fx[3], in1=x3[:, 0:3, 254:256],
                                                                op0=ALU.mult, op1=ALU.add), r=rp + [fixv, xc], w=[xc])
            S.op("dve", lambda: nc.vector.tensor_copy(out=xcb.ap, in_=xc.ap), r=[xc], w=[xcb])
            S.op("pe", proj_fm(wA, 512 + c * 128, hT, 2), r=[w_inA] + hT, w=[PB[2], PB[3]])
            S.op("act", lambda: nc.scalar.activation(out=ggr.ap, in_=pspan(2), func=GELU), r=[PB[2], PB[3]], w=[ggr])
            for d_ in range(2):
                r_, i_ = rgt[d_][0], rgt[d_][1]
                def mmg(d_=d_):
                    last = None
                    for kind in range(2):
                        col = (c * 4 + d_ * 2 + kind) * 128
                        for th in range(2):
                            last = nc.tensor.matmul(PB[4 + kind * 2 + th].ap, wgate.ap[:, col:col + 128],
                                                    xcb.ap[:, th * 512:(th + 1) * 512], start=True, stop=True)
                    return last
                S.op("pe", mmg, r=[wgate, xcb], w=[PB[4], PB[5], PB[6], PB[7]])
                S.op("act", lambda: nc.scalar.activation(out=r_.ap, in_=pspan(4), func=AF.Sigmoid,
                                                         bias=VC(V_BA + d_ * 4 + c)), r=[PB[4], PB[5], vecs], w=[r_])
                S.op("act", lambda: nc.scalar.activation(out=i_.ap, in_=pspan(6), func=AF.Sigmoid,
                                                         bias=VC(V_BX + d_ * 4 + c)), r=[PB[6], PB[7], vecs], w=[i_])
            for d_ in range(2):
                r_, a_, m_ = rgt[d_][0], rgt[d_][2], rgt[d_][3]
                S.op("act", lambda: nc.scalar.activation(out=a_.ap, in_=r_.ap, func=AF.Exp,
                                                         scale=clam.ap[:, d_ * 4 + c:d_ * 4 + c + 1]),
                     r=[r_, clam], w=[a_])
                S.op("act", lambda: nc.scalar.activation(out=m_.ap, in_=r_.ap, func=AF.Exp,
                                                         scale=clam.ap[:, 8 + d_ * 4 + c:8 + d_ * 4 + c + 1]),
                     r=[r_, clam], w=[m_])
                S.op("dve", lambda: nc.vector.tensor_scalar(out=m_.ap, in0=m_.ap, scalar1=-1.0, scalar2=0.0,
                                                             op0=ALU.add, op1=ALU.min), r=[m_], w=[m_])
            for d_ in range(2):
                m_ = rgt[d_][3]
                S.op("act", lambda: nc.scalar.activation(out=m_.ap, in_=m_.ap, func=AF.Sqrt, scale=-1.0),
                     r=[m_], w=[m_])
            for d_ in range(2):
                i_, a_, m_, h_ = rgt[d_][1], rgt[d_][2], rgt[d_][3], rgt[d_][4]
                S.op("dve", lambda: nc.vector.tensor_tensor(out=i_.ap, in0=i_.ap, in1=xc.ap, op=ALU.mult),
                     r=[i_, xc], w=[i_])
                S.op("dve", lambda: nc.vector.tensor_tensor(out=m_.ap, in0=m_.ap, in1=i_.ap, op=ALU.mult),
                     r=[m_, i_], w=[m_])
                bcols = a_.ap[:, 256:NT:256] if d_ == 0 else a_.ap[:, 255:NT - 1:256]
                S.op("dve", lambda: nc.vector.tensor_scalar(out=bcols, in0=bcols, scalar1=VC(V_MASK), scalar2=None,
                                                             op0=ALU.mult), r=[a_, vecs], w=[a_])
                if d_ == 0:
                    S.op("dve", lambda: nc.vector.tensor_tensor_scan(out=h_.ap, data0=a_.ap, data1=m_.ap,
                                                                      initial=VC(V_ST + c), op0=ALU.mult,
                                                                      op1=ALU.add), r=[a_, m_, vecs], w=[h_])
                    S.op("pool", lambda: nc.gpsimd.tensor_copy(out=st_t.ap[:, c * 8:c * 8 + 4],
                                                               in_=h_.ap[:, 255:NT:256]), r=[h_], w=[st_t])
                else:
                    S.op("dve", lambda: nc.vector.tensor_tensor_scan(out=h_.ap[:, ::-1], data0=a_.ap[:, ::-1],
                                                                      data1=m_.ap[:, ::-1],
                                                                      initial=VC(V_ST + 4 + c), op0=ALU.mult,
                                                                      op1=ALU.add), r=[a_, m_, vecs], w=[h_])
                    S.op("pool", lambda: nc.gpsimd.tensor_copy(out=st_t.ap[:, c * 8 + 4:c * 8 + 8],
                                                               in_=h_.ap[:, 0:NT:256]), r=[h_], w=[st_t])
            hf, hb = rgt[0][4], rgt[1][4]
            S.op("dve", lambda: nc.vector.tensor_tensor(out=hf.ap, in0=hf.ap, in1=hb.ap, op=ALU.add),
                 r=[hf, hb], w=[hf])
            S.op("dve", lambda: nc.vector.tensor_tensor(out=ycT[c].ap, in0=hf.ap, in1=ggr.ap, op=ALU.mult),
                 r=[hf, ggr], w=[ycT[c]])
        out_events.append(S.dma("sp", o_st, st_t.ap, r=[st_t]))
        for c in range(4):
            dump("yc%d" % c, ycT[c])

        o = M
        qlat = [T("qlat%d" % i, o + i * 4096, NT, F32) for i in range(3)]; o += 12288
        ckvf = [T("ckvf%d" % i, o + i * 4096, NT, F32) for i in range(2)]; o += 8192
        qn = [T("qn%d" % i, o + i * 2048, NT, BF16) for i in range(3)]; o += 6144
        ckva = [T("ckva%d" % i, o + i * 3072, 1536, BF16) for i in range(2)]; o += 6144
        Kb = [T("Kb%d" % i, o + i * 3072, 1536, BF16) for i in range(2)]; o += 6144
        Qb = [T("Qb%d" % i, o + i * 2048, NT, BF16) for i in range(2)]; o += 4096
        Pt = [T("Pt%d" % i, o + i * 1024, 512, BF16) for i in range(4)]; o += 4096
        rb = T("rb", o, 512, F32); o += 2048
        rtA = T("rtA", o, NT, F32, 64, 96); o += 4096
        rtB = T("rtB", o, NT, F32, 64, 96); o += 4096
        VT_EXTRA = o
        o += 4 * 1536 + 2048
        assert o <= M + 67584

        for rc in range(2):
            S.dma("pool", ckva[rc].ap[:, 0:512], d_ckvc[rc * 128:(rc + 1) * 128, :], w=[ckva[rc]])
        for i in range(2):
            S.op("pool", lambda: nc.gpsimd.memset(Kb[i].ap[96:128, :], 0.0), w=[Kb[i]])
            S.op("pool", lambda: nc.gpsimd.memset(Qb[i].ap[96:128, :], 0.0), w=[Qb[i]])
        mstg = T("mstg", rtA.b0 - M + M, 1536, F32, 96, 100)
        S.dma("sp", mstg.ap, d_maskk, w=[mstg])
        for i in range(2):
            S.op("pool", lambda: nc.gpsimd.tensor_copy(out=Kb[i].ap[96:100, :], in_=mstg.ap), r=[mstg], w=[Kb[i]])
        S.dma("sp", mstg.ap[:, 0:NT], d_maskq, w=[mstg])
        for i in range(2):
            S.op("pool", lambda: nc.gpsimd.tensor_copy(out=Qb[i].ap[96:100, :], in_=mstg.ap[:, 0:NT]),
                 r=[mstg], w=[Qb[i]])

        ssq = T("ssq", M + 142336, NT, F32)
        sqb = [T("sqb%d" % i, M + 146432 + i * 2048, NT, BF16) for i in range(2)]

        def rms_block(nchunks, col0, gcol, dst_f, ps_acc, ndim):
            for qc in range(nchunks):
                psb = (qc % 2) * 2
                S.op("pe", proj_fm(wB, col0 + qc * 128, hT, psb), r=[w_inB] + hT, w=[PB[psb], PB[psb + 1]])
                sq = sqb[qc % 2]
                S.op("act", lambda: nc.scalar.activation(out=dst_f[qc].ap, in_=pspan(psb), func=AF.Identity),
                     r=[PB[psb], PB[psb + 1]], w=[dst_f[qc]])
                S.op("act", lambda: nc.scalar.activation(out=sq.ap, in_=pspan(psb), func=AF.Square),
                     r=[PB[psb], PB[psb + 1]], w=[sq])
                def mm(qc=qc, sq=sq):
                    last = None
                    for th in range(2):
                        last = nc.tensor.matmul(PB[ps_acc + th].ap, ones_bf.ap, sq.ap[:, th * 512:(th + 1) * 512],
                                                start=(qc == 0), stop=(qc == nchunks - 1))
                    return last
                S.op("pe", mm, r=[sq, ones_bf], w=[PB[ps_acc], PB[ps_acc + 1]])
            S.op("act", lambda: nc.scalar.activation(out=ssq.ap, in_=pspan(ps_acc), func=AF.Sqrt,
                                                     bias=epsrms.ap[:, 0:1], scale=1.0 / ndim),
                 r=[PB[ps_acc], PB[ps_acc + 1], epsrms], w=[ssq])
            S.op("dve", lambda: nc.vector.reciprocal(out=ssq.ap, in_=ssq.ap), r=[ssq], w=[ssq])

        rms_block(3, 0, V_QG, qlat, 4, 384.0)
        for qc in range(3):
            S.op("dve", lambda: nc.vector.scalar_tensor_tensor(out=qn[qc].ap, in0=qlat[qc].ap, scalar=VC(V_QG + qc),
                                                                in1=ssq.ap, op0=ALU.mult, op1=ALU.mult),
                 r=[qlat[qc], vecs, ssq], w=[qn[qc]])
        rms_block(2, 384, V_KVG, ckvf, 6, 256.0)
        for rc in range(2):
            S.op("dve", lambda: nc.vector.scalar_tensor_tensor(out=ckvf[rc].ap, in0=ckvf[rc].ap,
                                                                scalar=VC(V_KVG + rc), in1=ssq.ap,
                                                                op0=ALU.mult, op1=ALU.mult),
                 r=[ckvf[rc], vecs, ssq], w=[ckvf[rc]])
            out_events.append(S.dma("sp", o_ckvT[rc * 128:(rc + 1) * 128, :], ckvf[rc].ap, r=[ckvf[rc]]))
            S.op("pool", lambda: nc.gpsimd.tensor_copy(out=ckva[rc].ap[:, 512:1536], in_=ckvf[rc].ap),
                 r=[ckvf[rc]], w=[ckva[rc]])
        wk3 = wkr.ap.rearrange("p (c n) -> p c n", c=8)
        wks3 = wkrs.ap.rearrange("p (c n) -> p c n", c=8)
        for wv_, psb, tl in ((wk3, 0, wkr), (wks3, 2, wkrs)):
            def mm(wv_=wv_, psb=psb):
                last = None
                for th in range(2):
                    for kc in range(8):
                        last = nc.tensor.matmul(PB[psb + th].ap[0:96, :], wv_[:, kc, :],
                                                hT[kc].ap[:, th * 512:(th + 1) * 512], start=(kc == 0), stop=(kc == 7))
                return last
            S.op("pe", mm, r=[tl] + hT, w=[PB[psb], PB[psb + 1]])
        S.op("act", lambda: nc.scalar.activation(out=krout.ap, in_=pspan(0)[64:96, :], func=AF.Identity),
             r=[PB[0], PB[1]], w=[krout])
        out_events.append(S.dma("sp", o_krT, krout.ap, r=[krout]))
        S.op("dve", lambda: nc.vector.tensor_tensor(out=rtA.ap, in0=pspan(0)[64:96, :], in1=ropeC.ap, op=ALU.mult),
             r=[PB[0], PB[1], ropeC], w=[rtA])
        S.op("dve", lambda: nc.vector.tensor_tensor(out=rtB.ap, in0=pspan(2)[64:96, :], in1=ropeS.ap, op=ALU.mult),
             r=[PB[2], PB[3], ropeS], w=[rtB])
        S.op("pool", lambda: nc.gpsimd.tensor_tensor(out=KRall.ap[:, 512:1536], in0=rtA.ap, in1=rtB.ap, op=ALU.add),
             r=[rtA, rtB], w=[KRall])
        for i in range(2):
            S.op("pool", lambda: nc.gpsimd.tensor_copy(out=Kb[i].ap[64:96, :], in_=KRall.ap), r=[KRall], w=[Kb[i]])

        wv3 = wuv.ap.rearrange("p (c n) -> p c n", c=2)
        Vt = [T("Vt%d" % i, (M + 75776 + i * 1536) if i < 8 else (VT_EXTRA + (i - 8) * 1536), 768, BF16)
              for i in range(12)]
        for kc in range(12):
            psb = kc % 2
            def mm(kc=kc, psb=psb):
                last = None
                for rc in range(2):
                    last = nc.tensor.matmul(PB[psb].ap, ckva[rc].ap[:, kc * 128:(kc + 1) * 128], wv3[:, rc, :],
                                            start=(rc == 0), stop=(rc == 1))
                return last
            S.op("pe", mm, r=[wuv] + ckva, w=[PB[psb]])
            V3 = Vt[kc].ap.rearrange("p (c x) -> p c x", c=4)
            P4 = PB[psb].ap.rearrange("p (c t e) -> p c t e", c=4, t=2)
            S.op("pool", lambda: nc.gpsimd.memset(V3[:, :, 64:128], 0.0), w=[Vt[kc]])
            S.op("act", lambda: nc.scalar.activation(out=V3[:, :, 0:64], in_=P4[:, :, 0, :], func=AF.Identity),
                 r=[PB[psb]], w=[Vt[kc]])
            S.op("act", lambda: nc.scalar.activation(out=V3[:, :, 128:192], in_=P4[:, :, 1, :], func=AF.Identity),
                 r=[PB[psb]], w=[Vt[kc]])
        ydT = [T("ydT%d" % c, M + c * 2048, NT, BF16) for c in range(4)]
        wq3 = wuq.ap.rearrange("p (c n) -> p c n", c=3)
        wqs3 = wuqs.ap.rearrange("p (c n) -> p c n", c=3)
        wk3_ = wuk.ap.rearrange("p (c n) -> p c n", c=2)
        def head_proj(h):
            kb, qb = Kb[h % 2], Qb[h % 2]
            groups = []

            def kgroup(kblk):
                def mm():
                    last = None
                    for rc in range(2):
                        last = nc.tensor.matmul(PB[7].ap[0:64, :], wk3_[:, rc, h * 64:(h + 1) * 64],
                                                ckva[rc].ap[:, kblk * 512:(kblk + 1) * 512],
                                                start=(rc == 0), stop=(rc == 1))
                    return last
                S.op("pe", mm, r=[wuk] + ckva, w=[PB[7]])
                S.op("dve", lambda: nc.vector.tensor_copy(out=kb.ap[0:64, kblk * 512:(kblk + 1) * 512],
                                                          in_=PB[7].ap[0:64, :]),
                     r=[PB[7]], w=[kb])

            def qgroup(th, sw):
                ts_ = slice(th * 512, (th + 1) * 512)
                wv_, tl = (wq3, wuq) if sw == 0 else (wqs3, wuqs)

                def mm():
                    last = None
                    for qc in range(3):
                        last = nc.tensor.matmul(PB[7].ap[0:96, :], wv_[:, qc, h * 96:(h + 1) * 96],
                                                qn[qc].ap[:, th * 512:(th + 1) * 512],
                                                start=(qc == 0), stop=(qc == 2))
                    return last
                S.op("pe", mm, r=[tl] + qn, w=[PB[7]])
                if sw == 0:
                    S.op("dve", lambda: nc.vector.tensor_copy(out=qb.ap[0:64, ts_], in_=PB[7].ap[0:64, :]),
                         r=[PB[7]], w=[qb])
                    S.op("dve", lambda: nc.vector.tensor_tensor(out=rtA.ap[:, ts_], in0=PB[7].ap[64:96, :],
                                                                 in1=ropeC.ap[:, ts_], op=ALU.mult),
                         r=[PB[7], ropeC], w=[rtA])
                else:
                    S.op("dve", lambda: nc.vector.tensor_tensor(out=rtB.ap[:, ts_], in0=PB[7].ap[64:96, :],
                                                                 in1=ropeS.ap[:, ts_], op=ALU.mult),
                         r=[PB[7], ropeS], w=[rtB])
                    S.op("pool", lambda: nc.gpsimd.tensor_tensor(out=qb.ap[64:96, ts_], in0=rtA.ap[:, ts_],
                                                                  in1=rtB.ap[:, ts_], op=ALU.add),
                         r=[rtA, rtB], w=[qb])
            for kblk in range(3):
                groups.append(lambda kblk=kblk: kgroup(kblk))
            for th in range(2):
                for sw in range(2):
                    groups.append(lambda th=th, sw=sw: qgroup(th, sw))
            return groups

        steps = [(h, th, kc) for h in range(8) for th in range(2) for kc in range(12)]
        LA = 2
        SBANKS = (0, 1, 2)

        def emit_S(i):
            h, th, kc = steps[i]
            kb, qb = Kb[h % 2], Qb[h % 2]
            sb = SBANKS[i % 3]
            S.op("pe", lambda: nc.tensor.matmul(PB[sb].ap, kb.ap[:, kc * 128:(kc + 1) * 128],
                                                qb.ap[:, th * 512:(th + 1) * 512], start=True, stop=True),
                 r=[kb, qb], w=[PB[sb]])

        def emit_rest(i):
            h, th, kc = steps[i]
            blk = h * 2 + th
            ts_ = slice(th * 512, (th + 1) * 512)
            sb = SBANKS[i % 3]
            pt = Pt[i % 4]
            ob = 3 + (blk % 2)
            db = 5 + (blk % 2)
            S.op("act", lambda: nc.scalar.activation(out=pt.ap, in_=PB[sb].ap, func=AF.Exp, scale=ATTN_SCALE),
                 r=[PB[sb]], w=[pt])
            hb_ = (h // 2) * 192
            if h % 2 == 0:
                vl = Vt[kc].ap[:, hb_:hb_ + 64]
                ol = ones_bf.ap[:, 0:64]
                prt = slice(0, 64)
            else:
                vl = Vt[kc].ap[:, hb_ + 64:hb_ + 192]
                ol = zo.ap
                prt = slice(0, 128)
            def mmpv():
                nc.tensor.matmul(PB[ob].ap[prt, :], vl, pt.ap, start=(kc == 0), stop=(kc == 11))
                return nc.tensor.matmul(PB[db].ap[prt, :], ol, pt.ap, start=(kc == 0), stop=(kc == 11))
            S.op("pe", mmpv, r=[Vt[kc], pt, ones_bf, zo], w=[PB[ob], PB[db]])
            if kc == 11:
                pr2 = slice(0, 64) if h % 2 == 0 else slice(64, 128)
                rb_ = rbs[blk % 2]
                S.op("dve", lambda: nc.vector.reciprocal(out=rb_.ap[pr2, :], in_=PB[db].ap[pr2, :]),
                     r=[PB[db]], w=[rb_])
                S.op("dve", lambda: nc.vector.tensor_tensor(out=ydT[h // 2].ap[pr2, ts_], in0=PB[ob].ap[pr2, :],
                                                             in1=rb_.ap[pr2, :], op=ALU.mult),
                     r=[PB[ob], rb_], w=[ydT[h // 2]])

        rbs = [rb, T("rb2", VT_EXTRA + 4 * 1536, 512, F32)]
        for g_ in head_proj(0):
            g_()
        pending = []
        for i in range(len(steps) + LA):
            if i < len(steps):
                h, th, kc = steps[i]
                if th == 0 and kc == 0:
                    assert not pending
                    pending = head_proj(h + 1) if h + 1 < 8 else []
                emit_S(i)
                if pending and (th * 12 + kc) % 3 == 1:
                    pending.pop(0)()
            if i - LA >= 0:
                emit_rest(i - LA)
        for c in range(4):
            dump("yd_c%d" % c, ydT[c])
        wO = w_out.ap.rearrange("p (c n) -> p c n", c=8)
        ysrc = ycT + ydT
        for n in range(8):
            psb = (n % 2) * 2
            S.op("pe", proj_fm(wO, n * 128, ysrc, psb), r=[w_out] + ysrc, w=[PB[psb], PB[psb + 1]])
            residual_evac(l, 16, n, psb)
        return END

    SLAB_BASE = ARENA - 32768 - 24576
    even_mixer(0, MIX_BASE, hook_setup=lambda off: mod_setup(1, off, 512),
               pre_hooks=mod0_pieces[4:], pre_finish=mod0_finish.b)
    hb_next(0)
    slabs0 = ffn_prefetch(0, SLAB_BASE)
    layernorm(0, 1, MIX_BASE)
    if stop_after == "mix0":
        for c in range(8):
            out_events.append(S.dma("sp", o_yT[c * 128:(c + 1) * 128, :], xT[c].ap, r=[xT[c]]))
    else:
        ffn(0, MIX_BASE, SLAB_BASE, slabs0)
        layernorm(0, 2, MIX_BASE)
        if stop_after == "ffn0":
            for c in range(8):
                out_events.append(S.dma("sp", o_yT[c * 128:(c + 1) * 128, :], xT[c].ap, r=[xT[c]]))
        else:
            odd_mixer(1, MIX_BASE)
            slabs1 = ffn_prefetch(1, SLAB_BASE)
            layernorm(1, 1, MIX_BASE)
            if stop_after == "mix1":
                for c in range(8):
                    out_events.append(S.dma("sp", o_yT[c * 128:(c + 1) * 128, :], xT[c].ap, r=[xT[c]]))
            else:
                ffn(1, MIX_BASE, SLAB_BASE, slabs1)
                layernorm(1, 2, MIX_BASE, final=True)
    if debug == "canary_end":
        out_events.append(S.dma("sp", o_dbg[:, 4400:4400 + NV], vecs.ap, r=[vecs]))
        out_events.append(S.dma("sp", o_dbg[:, 5000:5016], clam.ap, r=[clam]))
    S.wait_all("sp", out_events)
    build_program.nwaits = S.nwaits
    return nc


def _pcols(vec):
    vec = np.asarray(vec, np.float32).reshape(-1, 128)
    return np.ascontiguousarray(vec.T)


def _consts(kind):
    t = np.arange(NT)
    if kind == "prompt":
        seg = t // 256
        tau = t % 256
        L = 256
    else:
        seg = np.zeros(NT, np.int64)
        tau = t
        L = NT
    invcnt = np.zeros((4, NT), np.float32)
    for gi, w in enumerate((2, 4, 8, 16)):
        lo = np.clip(tau - w // 2, 0, L)
        hi = np.clip(tau + w - w // 2, 0, L)
        invcnt[gi] = 1.0 / (hi - lo).astype(np.float32)
    maskq = np.zeros((4, NT), np.float32)
    maskk = np.zeros((4, 1536), np.float32)
    if kind == "prompt":
        for s in range(4):
            maskq[s, seg == s] = 1.0
            maskk[s, :] = NEG
            maskk[s, 512 + np.nonzero(seg == s)[0]] = 0.0
    else:
        maskq[0, :] = 1.0
    ropeC = np.ones((32, NT), np.float32)
    ropeS = np.zeros((32, NT), np.float32)
    if kind == "sample":
        half = 16
        inv = (10000.0 ** (-np.arange(0, half, 2, dtype=np.float32) / half)).astype(np.float32)
        r = (t // 64).astype(np.float32)
        col = (t % 64).astype(np.float32)
        ang = np.concatenate([r[:, None] * inv, col[:, None] * inv], axis=-1).astype(np.float32)
        cos = np.cos(ang).astype(np.float32).T
        sin = np.sin(ang).astype(np.float32).T
        ropeC[0::2] = cos
        ropeC[1::2] = cos
        ropeS[0::2] = -sin
        ropeS[1::2] = sin
    return dict(invcnt=np.ascontiguousarray(np.broadcast_to(invcnt.reshape(1, 4096), (128, 4096))),
                maskq=maskq, maskk=maskk, ropeC=ropeC, ropeS=ropeS)


def prep_inputs(inp):
    f = lambda k: np.asarray(inp[k], np.float32)
    shared = {}
    shared["w_mod"] = f("w_mod")
    shared["even_w_in"] = f("even_w_in")[0]
    shared["even_w_out"] = f("even_w_out")[0]
    shared["w_sT"] = np.ascontiguousarray(f("even_w_s")[0].transpose(2, 0, 1).reshape(128, 512))
    shared["b_s"] = f("even_b_s")[0].reshape(1, 512)
    shared["w_pool"] = np.ascontiguousarray(f("even_w_pool")[0].transpose(1, 0, 2).reshape(128, 512))
    owin = f("odd_w_in")[0]
    shared["odd_w_in_a"] = np.ascontiguousarray(owin[:, 0:1024])
    shared["odd_w_in_b"] = np.ascontiguousarray(owin[:, 1024:1664])
    wkr = owin[:, 1664:1696]
    pad = np.zeros((D, 96), np.float32)
    pad[:, 64:96] = wkr
    shared["w_kr_pad"] = pad
    swap = np.arange(32) ^ 1
    pads = np.zeros((D, 96), np.float32)
    pads[:, 64:96] = wkr[:, swap]
    shared["w_kr_swap"] = pads
    wg = np.zeros((128, 4, 4, 128), np.float32)
    wa, wx = f("rg_w_a")[0], f("rg_w_x")[0]
    for c in range(4):
        for k, wsrc in enumerate((wa[0], wx[0], wa[1], wx[1])):
            wg[0:64, c, k, 0:64] = wsrc[2 * c]
            wg[64:128, c, k, 64:128] = wsrc[2 * c + 1]
    shared["w_gate"] = wg.reshape(128, 2048)
    wuq = f("mla_w_uq")[0]
    shared["w_uq"] = np.ascontiguousarray(wuq.reshape(384, 768))
    wuqs = np.zeros((384, 8, 96), np.float32)
    wuqs[:, :, 64:96] = wuq[:, :, 64 + swap]
    shared["w_uq_swap"] = wuqs.reshape(384, 768)
    shared["w_uk"] = np.ascontiguousarray(f("mla_w_uk")[0].reshape(256, 512))
    shared["w_uv"] = np.ascontiguousarray(f("mla_w_uv")[0].reshape(256, 512))
    shared["odd_w_out"] = f("odd_w_out")[0]
    wup = f("ffn_w_up")
    wupr = np.zeros((2, 6, D, 1024), np.float32)
    for s in range(5):
        wupr[:, s, :, 0:512] = wup[:, :, s * 512:(s + 1) * 512]
        wupr[:, s, :, 512:1024] = wup[:, :, D_FF + s * 512:D_FF + (s + 1) * 512]
    wupr[:, 5, :, 0:256] = wup[:, :, 2560:2816]
    wupr[:, 5, :, 256:512] = wup[:, :, D_FF + 2560:D_FF + 2816]
    shared["ffn_w_up_r"] = wupr
    shared["ffn_w_down"] = f("ffn_w_down")

    vec = np.zeros((128, NV), np.float32)
    bm = f("b_mod")
    for l in range(2):
        vec[:, V_BMOD + l * 48:V_BMOD + (l + 1) * 48] = _pcols(bm[l])
        for i, k in enumerate(("ln1_g", "ln1_b", "ln2_g", "ln2_b")):
            vec[:, V_LN + l * 32 + i * 8:V_LN + l * 32 + (i + 1) * 8] = _pcols(f(k)[l])
        fcw = f("ffn_conv_w")[l]
        for tap in range(3):
            vec[:, V_FCW + (l * 3 + tap) * 44:V_FCW + (l * 3 + tap + 1) * 44] = _pcols(fcw[tap])
        vec[:, V_FCB + l * 44:V_FCB + (l + 1) * 44] = _pcols(f("ffn_conv_b")[l])
    vec[:, V_PSC:V_PSC + 4] = _pcols(f("even_pool_scale")[0])
    rgw = f("rg_conv_w")[0]
    for tap in range(4):
        vec[:, V_RGW + tap * 4:V_RGW + (tap + 1) * 4] = _pcols(rgw[tap])
    vec[:, V_RGB:V_RGB + 4] = _pcols(f("rg_conv_b")[0])
    for d_ in range(2):
        vec[:, V_BA + d_ * 4:V_BA + (d_ + 1) * 4] = _pcols(f("rg_b_a")[0, d_])
        vec[:, V_BX + d_ * 4:V_BX + (d_ + 1) * 4] = _pcols(f("rg_b_x")[0, d_])
        vec[:, V_LAM + d_ * 4:V_LAM + (d_ + 1) * 4] = _pcols(f("rg_lam")[0, d_])
    vec[:, V_QG:V_QG + 3] = _pcols(f("mla_q_g")[0])
    vec[:, V_KVG:V_KVG + 2] = _pcols(f("mla_kv_g")[0])

    cp = _consts("prompt")
    cs = _consts("sample")
    xp, xs = f("x_prompt"), f("x_sample")
    in_maps = []
    for core in range(8):
        m = dict(shared)
        vv = vec.copy()
        if core < 4:
            xc = xp[4 * core:4 * core + 4].reshape(NT, D)
            cvv = f("c_ctx")
            m.update(cp)
            m["ckv_cacheT"] = np.zeros((256, 512), np.float32)
            m["kr_cacheT"] = np.zeros((32, 512), np.float32)
            vv[:, V_MASK] = 0.0
        else:
            b = core - 4
            xc = xs[b]
            cvv = f("c")[b]
            m.update(cs)
            m["ckv_cacheT"] = np.ascontiguousarray(f("cache_mla_ckv")[b, 0].T)
            m["kr_cacheT"] = np.ascontiguousarray(f("cache_mla_krope")[b, 0].T)
            st = f("state_rglru")[b, 0]
            for d_ in range(2):
                vv[:, V_ST + d_ * 4:V_ST + (d_ + 1) * 4] = _pcols(st[d_])
            vv[:, V_MASK] = 1.0
        m["xT"] = np.ascontiguousarray(xc.T)
        m["cv"] = _pcols(cvv)
        m["vecs"] = vv
        in_maps.append(m)
    return in_maps


def assemble(results):
    yp = np.zeros((16, 256, D), np.float32)
    ys = np.zeros((4, NT, D), np.float32)
    ckv = np.zeros((16, 1, 256, 256), np.float32)
    kr = np.zeros((16, 1, 256, 32), np.float32)
    st = np.zeros((16, 1, 2, 512), np.float32)
    for core in range(8):
        r = results[core]
        y = np.ascontiguousarray(r["yT"].T)
        if core < 4:
            yp[4 * core:4 * core + 4] = y.reshape(4, 256, D)
            ckv[4 * core:4 * core + 4, 0] = np.ascontiguousarray(r["ckvT"].T).reshape(4, 256, 256)
            kr[4 * core:4 * core + 4, 0] = np.ascontiguousarray(r["krT"].T).reshape(4, 256, 32)
            s_ = r["st"].reshape(128, 4, 2, 4)
            st[4 * core:4 * core + 4, 0] = s_.transpose(3, 2, 1, 0).reshape(4, 2, 512)
        else:
            ys[core - 4] = y
    return yp, ys, ckv, kr, st


_NC_CACHE = {}


def kernel(**inputs):
    in_maps = prep_inputs(inputs)
    if "nc" not in _NC_CACHE:
        _NC_CACHE["nc"] = build_program()
    res = run_bass_kernel_spmd(_NC_CACHE["nc"], in_maps, core_ids=list(range(8)))
    return assemble(res.results)
```
